# Optimizing a Trainium2 kernel written in Bass

```python
import jax, jax.numpy as jnp
from jax import lax
import numpy as np


D_MODEL = 1024
BATCH = 8
SEQ = 4096
DEPTH = 4

GRID_W = 64
CTX_LEN = 256
EPS = 1e-6

FOURIER_HEADS = 4
FOURIER_HEAD_DIM = D_MODEL // 8
FOURIER_WIDTH = FOURIER_HEADS * FOURIER_HEAD_DIM
HEAD_DIM = 64
N_Q_HEADS = (D_MODEL // 2) // HEAD_DIM
N_KV_HEADS = 2
GQA_GROUP = N_Q_HEADS // N_KV_HEADS
ATTN_WIDTH = N_Q_HEADS * HEAD_DIM
KV_WIDTH = N_KV_HEADS * HEAD_DIM
Q_END = FOURIER_WIDTH + ATTN_WIDTH
IN_WIDTH = Q_END + 2 * KV_WIDTH
MIX_WIDTH = FOURIER_WIDTH + ATTN_WIDTH
WINDOW = 128
BLOCK = 128
ROPE_THETA = 10000.0
POOL_WINDOWS = (2, 4, 8, 16)
POOL_GROUP = D_MODEL // len(POOL_WINDOWS)
FFN_HIDDEN = ((-(-8 * D_MODEL // 3)) + 255) // 256 * 256
N_EVEN = (DEPTH + 1) // 2
N_ODD = DEPTH // 2

kernel_name = "hybrid_fourier_window_pool_dit"


def rms_norm(x, g):
    xf = x.astype(jnp.float32)
    y = xf * lax.rsqrt(jnp.mean(xf * xf, axis=-1, keepdims=True) + EPS)
    return (y * g.astype(jnp.float32)).astype(x.dtype)


def ada_mod(cond, w, b):
    m = jax.nn.silu(cond) @ w + b
    return jnp.split(m[..., None, :], 6, axis=-1)


def modulate(h, shift, scale):
    return h * (1 + scale) + shift


def axial_rope_tables(n_tokens):
    rows = n_tokens // GRID_W
    row = jnp.repeat(jnp.arange(rows, dtype=jnp.float32), GRID_W)
    col = jnp.tile(jnp.arange(GRID_W, dtype=jnp.float32), rows)
    n_freq = HEAD_DIM // 4
    inv = ROPE_THETA ** (-jnp.arange(n_freq, dtype=jnp.float32) / n_freq)
    ang = jnp.concatenate([row[:, None] * inv[None], col[:, None] * inv[None]], axis=-1)
    return jnp.cos(ang), jnp.sin(ang)


def apply_rope(x, cos, sin):
    xf = x.astype(jnp.float32).reshape(*x.shape[:-1], HEAD_DIM // 2, 2)
    x1, x2 = xf[..., 0], xf[..., 1]
    c, s = cos[:, None, :], sin[:, None, :]
    out = jnp.stack([x1 * c - x2 * s, x1 * s + x2 * c], axis=-1).reshape(x.shape)
    return out.astype(x.dtype)


def fourier_mix(u):
    B, N, _ = u.shape
    uh = u.astype(jnp.float32).reshape(B, N, FOURIER_HEADS, FOURIER_HEAD_DIM)
    y = jnp.fft.fftn(uh, axes=(1, 3), norm='ortho').real
    return y.reshape(B, N, FOURIER_WIDTH).astype(u.dtype)


def sink_logits(sink, lead_shape):
    s = sink.astype(jnp.float32).reshape(1, N_KV_HEADS, GQA_GROUP, 1, 1)
    return jnp.broadcast_to(s, lead_shape + (1,))


def window_attention(q, k, v, k_ctx, v_ctx, sink):
    B, S = q.shape[:2]
    L = k_ctx.shape[1]
    nb = S // BLOCK
    scale = HEAD_DIM ** -0.5
    pad = ((0, 0), (BLOCK, BLOCK), (0, 0), (0, 0))
    kp, vp = jnp.pad(k, pad), jnp.pad(v, pad)
    qb = q.reshape(B, nb, BLOCK, N_KV_HEADS, GQA_GROUP, HEAD_DIM).transpose(1, 0, 2, 3, 4, 5)

    def one_block(args):
        i, q_i = args
        start = i * BLOCK
        k_i = lax.dynamic_slice_in_dim(kp, start, 3 * BLOCK, axis=1)
        v_i = lax.dynamic_slice_in_dim(vp, start, 3 * BLOCK, axis=1)
        qpos = start + jnp.arange(BLOCK)
        kpos = start - BLOCK + jnp.arange(3 * BLOCK)
        valid = (jnp.abs(kpos[None, :] - qpos[:, None]) <= WINDOW) & (kpos[None, :] >= 0) & (kpos[None, :] < S)
        s_win = jnp.einsum('bqkgd,bjkd->bkgqj', q_i, k_i).astype(jnp.float32) * scale
        s_win = jnp.where(valid[None, None, None], s_win, -jnp.inf)
        s_ctx = jnp.einsum('bqkgd,bckd->bkgqc', q_i, k_ctx).astype(jnp.float32) * scale
        logits = jnp.concatenate([s_win, s_ctx, sink_logits(sink, s_win.shape[:-1])], axis=-1)
        p = jax.nn.softmax(logits, axis=-1)
        p_win = p[..., :3 * BLOCK].astype(v.dtype)
        p_ctx = p[..., 3 * BLOCK:3 * BLOCK + L].astype(v.dtype)
        return (jnp.einsum('bkgqj,bjkd->bqkgd', p_win, v_i)
                + jnp.einsum('bkgqc,bckd->bqkgd', p_ctx, v_ctx))

    o = lax.map(one_block, (jnp.arange(nb), qb))
    return o.transpose(1, 0, 2, 3, 4, 5).reshape(B, S, ATTN_WIDTH)


def context_attention(q_c, k_c, v_c, sink):
    B, L = q_c.shape[:2]
    qg = q_c.reshape(B, L, N_KV_HEADS, GQA_GROUP, HEAD_DIM)
    s = jnp.einsum('bqkgd,bckd->bkgqc', qg, k_c).astype(jnp.float32) * HEAD_DIM ** -0.5
    p = jax.nn.softmax(jnp.concatenate([s, sink_logits(sink, s.shape[:-1])], axis=-1), axis=-1)
    p = p[..., :L].astype(v_c.dtype)
    return jnp.einsum('bkgqc,bckd->bqkgd', p, v_c).reshape(B, L, ATTN_WIDTH)


def even_mixer(xn, xcn, w_in, w_out, sink, cos, sin, with_ctx_out):
    B, S, _ = xn.shape
    L = xcn.shape[1]
    proj = xn @ w_in
    f_in = proj[..., :FOURIER_WIDTH]
    q = proj[..., FOURIER_WIDTH:Q_END].reshape(B, S, N_Q_HEADS, HEAD_DIM)
    k = proj[..., Q_END:Q_END + KV_WIDTH].reshape(B, S, N_KV_HEADS, HEAD_DIM)
    v = proj[..., Q_END + KV_WIDTH:].reshape(B, S, N_KV_HEADS, HEAD_DIM)
    q = apply_rope(q, cos, sin)
    k = apply_rope(k, cos, sin)
    kv_c = xcn @ w_in[:, Q_END:]
    k_c = kv_c[..., :KV_WIDTH].reshape(B, L, N_KV_HEADS, HEAD_DIM)
    v_c = kv_c[..., KV_WIDTH:].reshape(B, L, N_KV_HEADS, HEAD_DIM)
    attn = window_attention(q, k, v, k_c, v_c, sink)
    y = jnp.concatenate([fourier_mix(f_in), attn], axis=-1) @ w_out
    if not with_ctx_out:
        return y, None
    fq_c = xcn @ w_in[:, :Q_END]
    f_c = fq_c[..., :FOURIER_WIDTH]
    q_c = fq_c[..., FOURIER_WIDTH:].reshape(B, L, N_Q_HEADS, HEAD_DIM)
    attn_c = context_attention(q_c, k_c, v_c, sink)
    y_c = jnp.concatenate([fourier_mix(f_c), attn_c], axis=-1) @ w_out
    return y, y_c


def pool_mix(h, w_pool, scale):
    B, N, _ = h.shape
    hf = h.astype(jnp.float32)
    csum = jnp.pad(jnp.cumsum(hf, axis=1), ((0, 0), (1, 0), (0, 0)))
    t = np.arange(N)
    outs = []
    for g, w in enumerate(POOL_WINDOWS):
        lo = np.clip(t - w // 2, 0, N)
        hi = np.clip(t + w // 2, 0, N)
        sl = slice(g * POOL_GROUP, (g + 1) * POOL_GROUP)
        cg = csum[..., sl]
        count = jnp.asarray(hi - lo, dtype=jnp.float32)[None, :, None]
        outs.append((cg[:, hi] - cg[:, lo]) / count - hf[..., sl])
    y = jnp.stack(outs, axis=2).astype(h.dtype)
    y = jnp.einsum('bngc,gcd->bngd', y, w_pool).reshape(B, N, D_MODEL)
    return y * scale


def swiglu(h, w1, w3, w2):
    return (jax.nn.silu(h @ w1) * (h @ w3)) @ w2


def setup_inputs(seed: int = 0) -> dict:
    key = jax.random.key(seed)
    ks = jax.random.split(key, 18)
    f32 = jnp.float32

    def nrm(k, shape, s=1.0):
        return s * jax.random.normal(k, shape, f32)

    return {
        'x': nrm(ks[0], (BATCH, SEQ, D_MODEL)),
        'c': nrm(ks[1], (BATCH, D_MODEL)),
        'ctx': nrm(ks[2], (BATCH, CTX_LEN, D_MODEL)),
        'c_ctx': nrm(ks[3], (D_MODEL,)),
        'ada_w': nrm(ks[4], (DEPTH, D_MODEL, 6 * D_MODEL), 0.5 * D_MODEL ** -0.5),
        'ada_b': nrm(ks[5], (DEPTH, 6 * D_MODEL), 0.02),
        'norm_mix_g': 1.0 + nrm(ks[6], (DEPTH, D_MODEL), 0.05),
        'norm_ffn_g': 1.0 + nrm(ks[7], (DEPTH, D_MODEL), 0.05),
        'mix_in_w': nrm(ks[8], (N_EVEN, D_MODEL, IN_WIDTH), D_MODEL ** -0.5),
        'mix_out_w': nrm(ks[9], (N_EVEN, MIX_WIDTH, D_MODEL), MIX_WIDTH ** -0.5),
        'attn_sink': nrm(ks[10], (N_EVEN, N_Q_HEADS), 0.5),
        'pool_w': nrm(ks[11], (N_ODD, len(POOL_WINDOWS), POOL_GROUP, POOL_GROUP), POOL_GROUP ** -0.5),
        'pool_scale': 1.0 + nrm(ks[12], (N_ODD, D_MODEL), 0.1),
        'ffn_w1': nrm(ks[13], (DEPTH, D_MODEL, FFN_HIDDEN), D_MODEL ** -0.5),
        'ffn_w3': nrm(ks[14], (DEPTH, D_MODEL, FFN_HIDDEN), D_MODEL ** -0.5),
        'ffn_w2': nrm(ks[15], (DEPTH, FFN_HIDDEN, D_MODEL), FFN_HIDDEN ** -0.5),
        'final_g': 1.0 + nrm(ks[16], (D_MODEL,), 0.05),
    }


def reference(x, c, ctx, c_ctx, ada_w, ada_b, norm_mix_g, norm_ffn_g, mix_in_w, mix_out_w,
              attn_sink, pool_w, pool_scale, ffn_w1, ffn_w3, ffn_w2, final_g):
    n_tokens = x.shape[1]
    cos, sin = axial_rope_tables(n_tokens)
    last_ctx_reader = max(range(0, DEPTH, 2))
    h, hc = x, ctx
    for layer in range(DEPTH):
        sh1, sc1, g1, sh2, sc2, g2 = ada_mod(c, ada_w[layer], ada_b[layer])
        update_ctx = layer < last_ctx_reader
        need_ctx_in = update_ctx or (layer % 2 == 0 and layer <= last_ctx_reader)
        if need_ctx_in:
            csh1, csc1, cg1, csh2, csc2, cg2 = ada_mod(c_ctx, ada_w[layer], ada_b[layer])
            xcn = modulate(rms_norm(hc, norm_mix_g[layer]), csh1, csc1)
        xn = modulate(rms_norm(h, norm_mix_g[layer]), sh1, sc1)
        j = layer // 2
        if layer % 2 == 0:
            y, y_c = even_mixer(xn, xcn, mix_in_w[j], mix_out_w[j], attn_sink[j], cos, sin, update_ctx)
        else:
            y = pool_mix(xn, pool_w[j], pool_scale[j])
            y_c = pool_mix(xcn, pool_w[j], pool_scale[j]) if update_ctx else None
        h = h + g1 * y
        hn = modulate(rms_norm(h, norm_ffn_g[layer]), sh2, sc2)
        h = h + g2 * swiglu(hn, ffn_w1[layer], ffn_w3[layer], ffn_w2[layer])
        if update_ctx:
            hc = hc + cg1 * y_c
            hcn = modulate(rms_norm(hc, norm_ffn_g[layer]), csh2, csc2)
            hc = hc + cg2 * swiglu(hcn, ffn_w1[layer], ffn_w3[layer], ffn_w2[layer])
    return rms_norm(h, final_g)
```

```python
import os
import math
import numpy as np
import ml_dtypes
from contextlib import ExitStack
import concourse.bass as bass
import concourse.mybir as mybir
from concourse.bass_utils import run_bass_kernel_spmd

F32 = mybir.dt.float32
BF16 = mybir.dt.bfloat16
AF = mybir.ActivationFunctionType
ALU = mybir.AluOpType

D = 1024
S = 4096
L = 256
NT = S + L
DEPTH = 4
HID = 2816
NJ = HID // 128
EPS = 1e-6
POOL_W = (2, 4, 8, 16)
INW = 2176


class Op:
    __slots__ = ("eng", "fn", "deps", "dma", "needs_inc", "count", "didx", "name")


class Prog:
    ENGS = ("pe", "act", "dve", "pool", "sp")
    NDMA = {"sp": 14, "pool": 6, "act": 4}

    def __init__(self):
        self.streams = {e: [] for e in self.ENGS}
        self.lastw = {}
        self.readers = {}
        self.ndma = {e: 0 for e in self.ENGS}
        self.dmaops = {e: [] for e in self.ENGS}

    def add(self, eng, fn, r=(), w=(), dma=False, name="", extra=()):
        o = Op()
        o.eng, o.fn, o.dma, o.needs_inc, o.name = eng, fn, dma, False, name
        o.count = None
        o.didx = None
        deps = list(extra)
        for k in r:
            x = self.lastw.get(k)
            if x is not None:
                deps.append(x)
        for k in w:
            x = self.lastw.get(k)
            if x is not None:
                deps.append(x)
            deps.extend(self.readers.get(k, ()))
        dd = []
        seen = set()
        for d in deps:
            if id(d) in seen or d is o:
                continue
            seen.add(id(d))
            if (not d.dma) and d.eng == "pe" and eng == "pe" and not dma:
                continue
            dd.append(d)
            d.needs_inc = True
        o.deps = dd
        for k in r:
            self.readers.setdefault(k, []).append(o)
        for k in w:
            self.lastw[k] = o
            self.readers[k] = []
        if dma:
            o.didx = self.ndma[eng]
            self.ndma[eng] += 1
            self.dmaops[eng].append(o)
        self.streams[eng].append(o)
        return o

    def barrier(self):
        deps = []
        for e in self.ENGS:
            for o in reversed(self.streams[e]):
                if not o.dma and o.fn is not None:
                    deps.append(o)
                    break
            K = self.NDMA.get(e, 1)
            deps.extend(self.dmaops[e][-K:])
        for e in self.ENGS:
            self.add(e, None, extra=deps, name="barrier")

    def finalize(self):
        for e in self.ENGS:
            c = 0
            for o in self.streams[e]:
                if o.dma or o.fn is None:
                    continue
                if o.needs_inc:
                    c += 1
                    o.count = c

    def emit(self, eng, h, psem, dsem):
        waited = {}
        K = self.NDMA.get(eng, 1)
        for o in self.streams[eng]:
            for d in o.deps:
                if d.dma:
                    kk = self.NDMA[d.eng]
                    sem = dsem[d.eng][d.didx % kk]
                    val = 16 * (d.didx // kk + 1)
                    key = ("d", d.eng, d.didx % kk)
                else:
                    sem = psem[d.eng]
                    val = d.count
                    key = ("p", d.eng)
                if waited.get(key, 0) >= val:
                    continue
                waited[key] = val
                h.wait_ge(sem, val)
            if o.fn is None:
                continue
            if o.dma:
                if o.didx >= K:
                    sem = dsem[eng][o.didx % K]
                    val = 16 * (o.didx // K)
                    key = ("d", eng, o.didx % K)
                    if waited.get(key, 0) < val:
                        waited[key] = val
                        h.wait_ge(sem, val)
                inst = o.fn(h)
                inst.then_inc(dsem[eng][o.didx % K], 16)
            else:
                inst = o.fn(h)
                if o.needs_inc:
                    inst.then_inc(psem[eng], 1)

    def final_waits(self, h, dsem):
        for e in self.ENGS:
            n = self.ndma[e]
            if n == 0:
                continue
            K = self.NDMA[e]
            for s in range(min(K, n)):
                last = ((n - 1 - s) // K) * K + s
                h.wait_ge(dsem[e][s], 16 * (last // K + 1))


class Arena:
    def __init__(self, ap, nbytes):
        self.ap = ap
        self.n = nbytes
        self.top = 0
        self.peak = 0

    def alloc(self, free_shape, dt):
        esz = 4 if dt == F32 else 2
        cnt = 1
        for d in free_shape:
            cnt *= d
        nb = (cnt * esz + 63) // 64 * 64
        off = self.top
        if off + nb > self.n:
            raise RuntimeError("arena overflow: need %d have %d" % (off + nb, self.n))
        self.top += nb
        self.peak = max(self.peak, self.top)
        v = self.ap[:, off // 4:(off + nb) // 4]
        if dt != F32:
            v = v.bitcast(dt)
        v = v[:, 0:cnt]
        if len(free_shape) == 2:
            v = v.rearrange("p (a b) -> p a b", a=free_shape[0])
        elif len(free_shape) == 3:
            v = v.rearrange("p (a b c) -> p a b c", a=free_shape[0], b=free_shape[1])
        return v

    def mark(self):
        return self.top

    def release(self, m):
        self.top = m


_CONST = {}


def _bf(a):
    return np.ascontiguousarray(a.astype(ml_dtypes.bfloat16))


def host_consts():
    if _CONST:
        return _CONST
    c = {}
    c["ident_f"] = np.eye(128, dtype=np.float32)
    c["ident_b"] = _bf(np.eye(128, dtype=np.float32))
    c["onesm"] = _bf(np.full((128, 128), 1.0 / 1024, dtype=np.float32))
    kk = np.arange(128)[:, None]
    qq = np.arange(128)[None, :]
    c["maskL"] = _bf((qq <= kk).astype(np.float32))
    c["maskR"] = _bf((kk <= qq).astype(np.float32))
    ang = 2 * np.pi * (np.arange(128)[:, None] * np.arange(128)[None, :] % 128) / 128.0
    c["csc"] = _bf(np.stack([np.cos(ang), -np.sin(ang), np.sin(ang)], axis=1).astype(np.float32))
    alt = np.zeros((128, 2), dtype=np.float32)
    alt[:, 0] = np.where(np.arange(128) % 2 == 0, 1.0, -1.0)
    c["alt"] = _bf(alt)
    n = np.arange(S, dtype=np.int64)[:, None]
    dft = np.empty((16, S, 2, 256), dtype=ml_dtypes.bfloat16)
    for kg in range(16):
        k = (np.arange(256, dtype=np.int64) + kg * 256)[None, :]
        a = 2 * np.pi * ((n * k) % S).astype(np.float64) / S
        dft[kg, :, 0, :] = np.cos(a).astype(np.float32).astype(ml_dtypes.bfloat16)
        dft[kg, :, 1, :] = np.sin(a).astype(np.float32).astype(ml_dtypes.bfloat16)
    c["dft"] = dft
    n = np.arange(L, dtype=np.int64)[:, None]
    k = np.arange(L, dtype=np.int64)[None, :]
    a = 2 * np.pi * ((n * k) % L).astype(np.float64) / L
    d2 = np.stack([np.cos(a), np.sin(a)], axis=1).astype(np.float32)
    c["dft256"] = _bf(d2.reshape(2, 128, 2, 256).transpose(1, 0, 2, 3))
    rows = S // 64
    row = np.repeat(np.arange(rows, dtype=np.float32), 64)
    col = np.tile(np.arange(64, dtype=np.float32), rows)
    inv = (10000.0 ** (-np.arange(16, dtype=np.float32) / 16)).astype(np.float32)
    angr = np.concatenate([row[:, None] * inv[None], col[:, None] * inv[None]], axis=-1)
    cos = np.cos(angr).astype(np.float32)
    sin = np.sin(angr).astype(np.float32)
    p = np.arange(128)
    pair = (p % 64) // 2
    sign = np.where(p % 2 == 0, -1.0, 1.0).astype(np.float32)
    c["cosT"] = np.ascontiguousarray(cos[:, pair].T)
    c["sinT"] = np.ascontiguousarray((sin[:, pair] * sign[None, :]).T)
    invc = np.ones((4, 2, 8), dtype=np.float32)
    for wi, w in enumerate(POOL_W):
        for t in range(w // 2):
            invc[wi, 0, t] = 1.0 / (t + w // 2)
            invc[wi, 1, t] = 1.0 / (w - t)
    c["invc"] = np.ascontiguousarray(np.broadcast_to(invc[None], (128, 4, 2, 8)))
    band = np.zeros((128, 4, 5, 128), dtype=np.float32)
    Nq = 512
    tq = np.arange(Nq)
    for wi, w in enumerate(POOL_W):
        M = np.zeros((Nq, Nq), dtype=np.float64)
        for tp in range(Nq):
            lo = max(tp - w // 2, 0)
            hi = min(tp + w // 2, Nq)
            M[lo:hi, tp] = 1.0 / (hi - lo)
            M[tp, tp] -= 1.0
        band[:, wi, 0, :] = M[128:256, 128:256]
        band[:, wi, 1, :] = M[0:128, 128:256]
        band[:, wi, 2, :] = M[256:384, 128:256]
        band[:, wi, 3, :] = M[0:128, 0:128]
        band[:, wi, 4, :] = M[384:512, 384:512]
    c["band"] = _bf(band)
    _CONST.update(c)
    return c


def in_ext_cols():
    f = list(range(0, 512))
    q = list(range(512, 1024))
    sw = lambda c: c + 1 if c % 2 == 0 else c - 1
    qs = [sw(c) for c in q]
    k0 = list(range(1024, 1088))
    k1 = list(range(1088, 1152))
    k0s = [sw(c) for c in k0]
    k1s = [sw(c) for c in k1]
    v = list(range(1152, 1280))
    cols = f + q + qs + k0 + k0 + k1 + k1 + k0s + k0s + k1s + k1s + v
    assert len(cols) == INW
    return np.array(cols)


def fm(vec):
    a = np.asarray(vec, dtype=np.float32)
    lead = a.shape[:-1]
    a = a.reshape(lead + (8, 128))
    a = np.moveaxis(a, -1, 0)
    return np.ascontiguousarray(a)


def build(nlayers=DEPTH, enable_mix=True, enable_ffn=True, mix_layers=(0, 1, 2, 3)):
    nc = bass.Bass("TRN2", target_bir_lowering=False)

    def din(name, shape, dt=F32):
        return nc.dram_tensor(name, list(shape), dt, kind="ExternalInput").ap()

    def dscr(name, shape, dt):
        return nc.dram_tensor(name, list(shape), dt, kind="Internal").ap()

    x_d = din("x", [S, D])
    ctx_d = din("ctx", [L, D])
    cvec_d = din("cvec", [128, 8, 2])
    adaw_d = din("ada_w", [DEPTH, D, 6 * D])
    adab_d = din("adab", [128, DEPTH, 48])
    gmix_d = din("gmix", [128, DEPTH, 8])
    gffn_d = din("gffn", [128, DEPTH, 8])
    gfin_d = din("gfin", [128, 8])
    win_d = din("w_in", [2, D, INW])
    wout_d = din("w_out", [2, D, D])
    sink_d = din("sink", [128, 2, 8])
    poolw_d = din("pool_w", [2, 4, 256, 256])
    pscale_d = din("pscale", [128, 2, 8])
    w1_d = din("ffn_w1", [DEPTH, D, HID])
    w3_d = din("ffn_w3", [DEPTH, D, HID])
    w2_d = din("ffn_w2", [DEPTH, HID, D])
    identf_d = din("ident_f", [128, 128])
    identb_d = din("ident_b", [128, 128], BF16)
    onesm_d = din("onesm", [128, 128], BF16)
    maskL_d = din("maskL", [128, 128], BF16)
    maskR_d = din("maskR", [128, 128], BF16)
    csc_d = din("csc", [128, 3, 128], BF16)
    alt_d = din("alt", [128, 2], BF16)
    dft_d = din("dft", [16, S, 2, 256], BF16)
    dft256_d = din("dft256", [128, 2, 2, 256], BF16)
    cosT_d = din("cosT", [128, S])
    sinT_d = din("sinT", [128, S])
    invc_d = din("invc", [128, 4, 2, 8])
    band_d = din("band", [128, 4, 5, 128], BF16)
    out_d = nc.dram_tensor("out", [S, D], F32, kind="ExternalOutput").ap()

    hT = [dscr("hT0", [8, 128, NT], F32), dscr("hT1", [8, 128, NT], F32)]
    wb1 = dscr("wb1", [DEPTH, NJ, 128, 1024], BF16)
    wb3 = dscr("wb3", [DEPTH, NJ, 128, 1024], BF16)
    wb2 = dscr("wb2", [DEPTH, 8, 128, HID], BF16)
    wbin = dscr("wbin", [2, 128, 8 * INW], BF16)
    wbout = dscr("wbout", [2, 128, 8 * D], BF16)
    wbpool = dscr("wbpool", [2, 128, 8 * 256], BF16)

    P = Prog()
    st = ExitStack()
    ARENA_F32 = 53000
    arena_t = st.enter_context(nc.sbuf_tensor("arena", [128, ARENA_F32], F32))
    A = Arena(arena_t[:], ARENA_F32 * 4)
    ps_all = st.enter_context(nc.psum_tensor("ps_all", [128, 8 * 512], F32))
    psb = [ps_all[:, i * 512:(i + 1) * 512] for i in range(8)]
    psem = {e: st.enter_context(nc.semaphore("ps_" + e)) for e in Prog.ENGS}
    dsem = {e: [st.enter_context(nc.semaphore("ds_%s%d" % (e, i))) for i in range(k)] for e, k in Prog.NDMA.items()}

    def PS(b):
        return ("ps", b)

    def dma(out, in_, r=(), w=(), q="sp"):
        return P.add(q, lambda e: e.dma_start(out=out, in_=in_), r=r, w=w, dma=True)

    def act(out, in_, func, r=(), w=(), bias=None, scale=None):
        kw = {}
        if bias is not None:
            kw["bias"] = bias
        if scale is not None:
            kw["scale"] = scale
        return P.add("act", lambda e: e.activation(out=out, in_=in_, func=func, **kw), r=r, w=w)

    def tt(eng, out, in0, in1, op, r=(), w=()):
        return P.add(eng, lambda e: e.tensor_tensor(out, in0, in1, op), r=r, w=w)

    def stt(eng, out, in0, scalar, in1, op0, op1, r=(), w=()):
        return P.add(eng, lambda e: e.scalar_tensor_tensor(out, in0, scalar, in1, op0, op1), r=r, w=w)

    def ts(eng, out, in0, s1, s2, op0, op1=None, r=(), w=()):
        if op1 is None:
            return P.add(eng, lambda e: e.tensor_scalar(out, in0, s1, None, op0), r=r, w=w)
        return P.add(eng, lambda e: e.tensor_scalar(out, in0, s1, s2, op0, op1), r=r, w=w)

    def cp(eng, out, in_, r=(), w=()):
        if eng == "act":
            return P.add("act", lambda e: e.copy(out, in_), r=r, w=w)
        return P.add(eng, lambda e: e.tensor_copy(out, in_), r=r, w=w)

    marks = []
    pe_count = [0]

    def mark(name):
        marks.append((name, pe_count[0]))

    def mm_group(items, r=(), w=()):
        pe_count[0] += len(items)
        def f(e):
            i = None
            for (o, l, rr, s0, s1) in items:
                i = e.matmul(o, l, rr, start=s0, stop=s1)
            return i
        return P.add("pe", f, r=r, w=w)

    ident_f = A.alloc([128], F32)
    ident_b = A.alloc([128], BF16)
    onesm = A.alloc([128], BF16)
    maskL = A.alloc([128], BF16)
    maskR = A.alloc([128], BF16)
    csc = A.alloc([3, 128], BF16)
    altv = A.alloc([2], BF16)
    dft256 = A.alloc([2, 2, 256], BF16)
    cvec = A.alloc([8, 2], F32)
    scb = A.alloc([8, 2], BF16)
    adab = A.alloc([DEPTH, 48], F32)
    gmix = A.alloc([DEPTH, 8], F32)
    gffn = A.alloc([DEPTH, 8], F32)
    gfin = A.alloc([8], F32)
    sink = A.alloc([2, 8], F32)
    esink = A.alloc([2, 8], F32)
    pscale = A.alloc([2, 8], F32)
    invc = A.alloc([4, 2, 8], F32)
    modT = [A.alloc([2, 48], F32) for _ in range(DEPTH)]
    gs1 = [A.alloc([2, 8], F32) for _ in range(DEPTH)]
    gs2 = [A.alloc([2, 8], F32) for _ in range(DEPTH)]
    gp = [A.alloc([2, 8], F32) for _ in range(DEPTH)]
    HW = 528
    hin = A.alloc([8, HW], F32)
    sq = A.alloc([8, HW], BF16)
    rsb = [A.alloc([HW], F32) for _ in range(2)]
    tbuf = [A.alloc([HW], F32) for _ in range(3)]
    PREP_N = 1408
    stg_f = [A.alloc([PREP_N], F32) for _ in range(2)]
    stg_b = [A.alloc([PREP_N], BF16) for _ in range(2)]

    for (sbt, drt, key) in [(ident_f, identf_d, "ident_f"), (ident_b, identb_d, "ident_b"), (onesm, onesm_d, "onesm"),
                            (maskL, maskL_d, "maskL"), (maskR, maskR_d, "maskR"), (csc, csc_d, "csc"), (altv, alt_d, "alt"),
                            (dft256, dft256_d, "dft256"), (cvec, cvec_d, "cvec"), (adab, adab_d, "adab"),
                            (gmix, gmix_d, "gmix"), (gffn, gffn_d, "gffn"), (gfin, gfin_d, "gfin"),
                            (sink, sink_d, "sink"), (pscale, pscale_d, "pscale"), (invc, invc_d, "invc")]:
        dma(sbt, drt, w=[key])
    act(esink, sink, AF.Exp, r=["sink"], w=["esink"])

    prep_q = []
    prep_state = {"i": 0, "n": 0, "pending": None}

    def prep_slab(src_ap, dst_ap, shape, key):
        cnt = 1
        for d_ in shape:
            cnt *= d_

        def A_(k):
            sf = stg_f[k][:, 0:cnt]
            sbv = stg_b[k][:, 0:cnt]
            sf3 = sf.rearrange("p (a b) -> p a b", a=shape[0])
            dma(sf3, src_ap, w=[("stgf", k)])
            cp("act", sbv, sf, r=[("stgf", k)], w=[("stgb", k)])

        def B_(k):
            sb3 = stg_b[k][:, 0:cnt].rearrange("p (a b) -> p a b", a=shape[0])
            dma(dst_ap, sb3, r=[("stgb", k)], w=[key])
        prep_q.append(["slab", key, A_, B_])

    def queue_ada(l, part):
        awv = adaw_d[l].rearrange("(kc p) n -> p kc n", p=128)
        rng_ = range(0, 16) if part == 1 else range(16, 48)
        for pc in rng_:
            def A_(k, pc=pc):
                sf = stg_f[k][:, 0:1024]
                dma(sf.rearrange("p (a b) -> p a b", a=8), awv[:, :, pc * 128:(pc + 1) * 128], w=[("stgf", k)])
                cp("act", stg_b[k][:, 0:1024], sf, r=[("stgf", k)], w=[("stgb", k)])

            def B_(k, pc=pc):
                wv = stg_b[k][:, 0:1024].rearrange("p (a b) -> p a b", a=8)
                mm_group([(psb[7][:, 0:2], wv[:, kc, :], scb[:, kc, :], kc == 0, kc == 7) for kc in range(8)],
                         r=[("stgb", k), "scb"], w=[PS(7)])
                tt("dve", modT[l][:, :, pc], psb[7][:, 0:2], adab[:, l, pc:pc + 1].to_broadcast([128, 2]), ALU.add,
                   r=["adab"], w=[PS(7), ("mod", l, part)])
            prep_q.append(["ada", ("modp", l, part), A_, B_])

        def Afin(k):
            pass

        def Bfin(k):
            for col in range(2):
                if part == 1:
                    stt("dve", gs1[l][:, col, :], modT[l][:, col, 8:16], 1.0, gmix[:, l, :], ALU.add, ALU.mult, r=["gmix"], w=[("mod", l, part)])
                else:
                    stt("dve", gs2[l][:, col, :], modT[l][:, col, 32:40], 1.0, gffn[:, l, :], ALU.add, ALU.mult, r=["gffn"], w=[("mod", l, part)])
                    if l % 2 == 1:
                        tt("dve", gp[l][:, col, :], modT[l][:, col, 16:24], pscale[:, l // 2, :], ALU.mult, r=["pscale"], w=[("mod", l, part)])
        prep_q.append(["adafin", ("modv", l, part), Afin, Bfin])

    def queue_prep_layer_mix(l):
        j = l // 2
        if l % 2 == 0:
            wv = win_d[j].rearrange("(kc p) n -> p kc n", p=128)
            dv = wbin[j].rearrange("p (kc n) -> p kc n", kc=8)
            for s_ in range(INW // 128):
                prep_slab(wv[:, :, s_ * 128:(s_ + 1) * 128], dv[:, :, s_ * 128:(s_ + 1) * 128], [8, 128], ("wbin", j))
            wv = wout_d[j].rearrange("(kc p) n -> p kc n", p=128)
            dv = wbout[j].rearrange("p (kc n) -> p kc n", kc=8)
            for s_ in range(8):
                prep_slab(wv[:, :, s_ * 128:(s_ + 1) * 128], dv[:, :, s_ * 128:(s_ + 1) * 128], [8, 128], ("wbout", j))
        else:
            wv = poolw_d[j].rearrange("g (kc p) d -> p (g kc) d", p=128)
            dv = wbpool[j].rearrange("p (a d) -> p a d", a=8)
            for s_ in range(2):
                prep_slab(wv[:, s_ * 4:(s_ + 1) * 4, :], dv[:, s_ * 4:(s_ + 1) * 4, :], [4, 256], ("wbpool", j))

    def queue_prep_layer_ffn(l):
        w1v = w1_d[l].rearrange("(kc p) n -> p kc n", p=128)
        w3v = w3_d[l].rearrange("(kc p) n -> p kc n", p=128)
        for j in range(NJ):
            prep_slab(w1v[:, :, j * 128:(j + 1) * 128], wb1[l, j].rearrange("p (kc n) -> p kc n", kc=8), [8, 128], ("wb1", l, j))
            prep_slab(w3v[:, :, j * 128:(j + 1) * 128], wb3[l, j].rearrange("p (kc n) -> p kc n", kc=8), [8, 128], ("wb3", l, j))
        w2v = w2_d[l].rearrange("(jc p) n -> p jc n", p=128)
        for d_ in range(8):
            dv = wb2[l, d_].rearrange("p (jc n) -> p jc n", jc=NJ)
            for hf in range(2):
                prep_slab(w2v[:, hf * 11:(hf + 1) * 11, d_ * 128:(d_ + 1) * 128], dv[:, hf * 11:(hf + 1) * 11, :], [11, 128], ("wb2", l, d_, hf))

    prep_state["nslots"] = 2
    prep_state["lag"] = 1
    prep_state["pend"] = []

    def prep_flush():
        while prep_state["pend"]:
            t_, k_ = prep_state["pend"].pop(0)
            t_[3](k_)

    def prep_tick(n=1, allow_ada=True):
        for _ in range(n):
            i_ = prep_state["i"]
            nxt = prep_q[i_] if i_ < len(prep_q) else None
            if nxt is not None and nxt[0] in ("ada", "adafin") and not allow_ada:
                nxt = None
            if nxt is not None:
                k = prep_state["n"] % prep_state["nslots"]
                prep_state["n"] += 1
                nxt[2](k)
                prep_state["pend"].append((nxt, k))
                prep_state["i"] += 1
                while len(prep_state["pend"]) > prep_state["lag"]:
                    t_, k_ = prep_state["pend"].pop(0)
                    t_[3](k_)
            elif prep_state["pend"]:
                t_, k_ = prep_state["pend"].pop(0)
                t_[3](k_)

    def prep_require(key):
        last = -1
        for i_, t_ in enumerate(prep_q):
            if t_[1] == key:
                last = i_
        while prep_state["i"] <= last:
            prep_tick()
        idxs = [q_ for q_, (t_, _) in enumerate(prep_state["pend"]) if t_[1] == key]
        if idxs:
            for _ in range(idxs[-1] + 1):
                t_, k_ = prep_state["pend"].pop(0)
                t_[3](k_)

    act(scb, cvec, AF.Silu, r=["cvec"], w=["scb"])
    for l in range(nlayers):
        queue_ada(l, 1)
        if enable_mix and l in mix_layers:
            queue_prep_layer_mix(l)
        queue_ada(l, 2)
        if enable_ffn:
            queue_prep_layer_ffn(l)

    def norm_s1(n, hkey="hin", col0=0, hbuf=None, sqbuf=None, sqkeys=None):
        if hbuf is None:
            hbuf = hin
        hv = hbuf[:, :, col0:col0 + n]
        if sqbuf is None:
            sqb, sqk = sq, ["sq"]
        else:
            sqb, sqk = sqbuf, sqkeys
        sqv = sqb[:, :, col0:col0 + n]
        act(sqv, hv, AF.Square, r=[hkey], w=sqk)
        mm_group([(psb[0][:, 0:n], onesm, sqb[:, c, col0:col0 + n], c == 0, c == 7) for c in range(8)],
                 r=sqk + ["onesm"], w=[PS(0)])
        k = norm_mod.cnt % 2
        norm_mod.cnt += 1
        rs = rsb[k][:, 0:n]
        act(rs, psb[0][:, 0:n], AF.Sqrt, w=[PS(0), ("rs", k)], bias=EPS, scale=1.0)
        P.add("dve", lambda e: e.reciprocal(rs, rs), w=[("rs", k)])
        return rs, k

    def norm_s2(n, gsv, shv, out_fn, out_keys, rs, k, chunks, hkey="hin", col0=0, hbuf=None, vkey=None):
        if hbuf is None:
            hbuf = hin
        vr = [vkey] if vkey is not None else []
        for c in chunks:
            tk = norm_mod.tc % 3
            norm_mod.tc += 1
            tv = tbuf[tk][:, 0:n]
            stt("dve", tv, hbuf[:, c, col0:col0 + n], gsv[:, c:c + 1], rs, ALU.mult, ALU.mult,
                r=[hkey, ("rs", k)] + vr, w=[("tb", tk)])
            o = out_fn(c)
            ok_ = out_keys[c] if isinstance(out_keys, list) else out_keys
            if shv is None:
                cp("act", o, tv, r=[("tb", tk)], w=[ok_])
            else:
                act(o, tv, AF.Identity, r=[("tb", tk)] + vr, w=[ok_], bias=shv[:, c:c + 1], scale=1.0)

    def norm_mod(n, gsv, shv, out_fn, out_keys, hkey="hin", col0=0, hbuf=None, vkey=None, sqbuf=None, sqkeys=None):
        rs, k = norm_s1(n, hkey=hkey, col0=col0, hbuf=hbuf, sqbuf=sqbuf, sqkeys=sqkeys)
        norm_s2(n, gsv, shv, out_fn, out_keys, rs, k, range(8), hkey=hkey, col0=col0, hbuf=hbuf, vkey=vkey)
    norm_mod.cnt = 0
    norm_mod.tc = 0

    def hview(buf, t0, n):
        return buf.rearrange("c p t -> p c t")[:, :, t0:t0 + n]

    def hk(b, g):
        return [("hT", b, g, d_) for d_ in range(8)]

    groups = [(g * 512, 512) for g in range(8)] + [(S, L)]

    mark('ada')
    m0 = A.mark()
    xt = [A.alloc([4, D], F32) for _ in range(2)]
    hin2p = A.alloc([8, HW], F32)
    for _ in range(4):
        stg_f.append(A.alloc([PREP_N], F32))
        stg_b.append(A.alloc([PREP_N], BF16))
    prep_state["nslots"] = 6
    prep_state["lag"] = 4
    def p0_load(gi):
        t0, n = groups[gi]
        k = gi % 2
        src = x_d[t0:t0 + n, :] if t0 < S else ctx_d[:, :]
        dma(xt[k][:, 0:n // 128, :], src.rearrange("(t p) f -> p t f", p=128), w=[("xt", k)])

    p0_load(0)
    p0c = 0
    for gi, (t0, n) in enumerate(groups):
        k = gi % 2
        hb, hkey_ = (hin, "hin") if k == 0 else (hin2p, "hin2")
        nt_ = n // 128
        if gi + 1 < len(groups):
            p0_load(gi + 1)
        for t in range(nt_):
            for half in range(2):
                b = 1 + p0c % 4
                p0c += 1
                def f(e, k=k, t=t, half=half, b=b):
                    i = None
                    for c in range(4):
                        i = e.transpose(psb[b][:, c * 128:(c + 1) * 128], xt[k][:, t, (half * 4 + c) * 128:(half * 4 + c + 1) * 128], ident_f)
                    return i
                pe_count[0] += 4
                P.add("pe", f, r=[("xt", k), "ident_f"], w=[PS(b)])
                cp("dve" if half == 0 else "act", hb[:, half * 4:(half + 1) * 4, t * 128:(t + 1) * 128],
                   psb[b][:, :].rearrange("p (c n) -> p c n", c=4), w=[PS(b), hkey_])
            prep_tick(1)
        dma(hview(hT[0], t0, n), hb[:, :, 0:n], r=[hkey_], w=hk(0, gi))
        prep_tick(1)
    prep_require(("modv", 0, 1))
    if enable_mix and 0 in mix_layers:
        prep_require(("wbin", 0))
    prep_flush()
    P.barrier()
    A.release(m0)
    del stg_f[2:]
    del stg_b[2:]
    prep_state["nslots"] = 2
    prep_state["lag"] = 1
    prep_state["n"] = 0


    cur = 0

    def ffn_phase(l, with_ctx):
        mark('ffn%d' % l)
        mF = A.mark()
        prep_require(("modv", l, 2))
        vk = ("mod", l, 2)
        blocks = [[0, 1], [2, 3], [4, 5], [6, 7] + ([8] if with_ctx else [])]
        maxtok = max(sum(groups[g][1] for g in b) for b in blocks)
        hn2 = [A.alloc([8, maxtok], BF16) for _ in range(2)]
        hid = A.alloc([NJ, maxtok], BF16)
        wbuf = [A.alloc([2, 8, 128], BF16) for _ in range(3)]
        w2buf = [A.alloc([NJ, 128], BF16) for _ in range(2)]
        sab = [A.alloc([512], F32) for _ in range(3)]
        hres = [A.alloc([512], F32) for _ in range(6)]
        hT_c = hT[cur]
        cnt = {"s": 0, "o": 0}

        def offs_of(blk):
            offs, slot, o_ = {}, {}, 0
            for si, g in enumerate(blk):
                offs[g] = o_
                slot[g] = si
                o_ += groups[g][1]
            return offs, slot

        def f0_load(g):
            t0, n = groups[g]
            dma(hin[:, :, 0:n], hview(hT_c, t0, n), r=hk(cur, g), w=["hin"])

        f0_state = {}

        def f0_stage(bi, g, stage):
            blk = blocks[bi]
            offs, slot = offs_of(blk)
            hn = hn2[bi % 2]
            t0, n = groups[g]
            col = 1 if g == 8 else 0
            if stage == 0:
                f0_state[g] = norm_s1(n)
                return
            rs, k = f0_state[g]
            norm_s2(n, gs2[l][:, col, :], modT[l][:, col, 24:32],
                    lambda c: hn[:, c, offs[g]:offs[g] + n], [("hn", bi % 2, slot[g], c) for c in range(8)], rs, k,
                    range(0, 4) if stage == 1 else range(4, 8), vkey=vk)

        def f0_group(bi, g, load=True):
            blk = blocks[bi]
            offs, slot = offs_of(blk)
            hn = hn2[bi % 2]
            t0, n = groups[g]
            col = 1 if g == 8 else 0
            if load:
                f0_load(g)
            norm_mod(n, gs2[l][:, col, :], modT[l][:, col, 24:32],
                     lambda c: hn[:, c, offs[g]:offs[g] + n], [("hn", bi % 2, slot[g], c) for c in range(8)], vkey=vk)

        def load_w13(j):
            prep_require(("wb1", l, j))
            prep_require(("wb3", l, j))
            k = j % 3
            dma(wbuf[k][:, 0, :, :], wb1[l, j].rearrange("p (kc n) -> p kc n", kc=8), r=[("wb1", l, j)], w=[("wbuf", k, 0)])
            dma(wbuf[k][:, 1, :, :], wb3[l, j].rearrange("p (kc n) -> p kc n", kc=8), r=[("wb3", l, j)], w=[("wbuf", k, 1)])

        def load_w2(d_):
            prep_require(("wb2", l, d_, 0))
            prep_require(("wb2", l, d_, 1))
            k = d_ % 2
            dma(w2buf[k], wb2[l, d_].rearrange("p (jc n) -> p jc n", jc=NJ), r=[("wb2", l, d_, 0), ("wb2", l, d_, 1)], w=[("w2buf", k)])

        def load_hres(blk, slot, d_):
            for g in blk:
                t0, n = groups[g]
                hk_ = (d_ % 2) * 3 + slot[g]
                dma(hres[hk_][:, 0:n], hT_c[d_, :, t0:t0 + n], r=[("hT", cur, g, d_)], w=[("hres", hk_)])

        mark('ffn%d_b0_F0' % l)
        for g in blocks[0]:
            f0_group(0, g)
        load_w13(0)
        load_w13(1)
        for bi, blk in enumerate(blocks):
            offs, slot = offs_of(blk)
            hn = hn2[bi % 2]
            nxt = blocks[bi + 1] if bi + 1 < len(blocks) else []
            nxt_at = {}
            nxt_ld = {}
            for i_, g in enumerate(nxt):
                for st_ in range(3):
                    nxt_at[4 + 5 * i_ + st_] = (g, st_)
                nxt_ld[2 + 5 * i_] = g
            mark('ffn%d_b%d_F1' % (l, bi))
            for j in range(NJ):
                if j + 2 < NJ:
                    load_w13(j + 2)
                elif j + 2 == NJ:
                    load_w2(0)
                    load_hres(blk, slot, 0)
                k = j % 3
                for g in blk:
                    t0, n = groups[g]
                    s_ = cnt["s"] % 3
                    cnt["s"] += 1
                    pa, pb_ = ((1, 3), (2, 4), (5, 6))[s_]
                    hv = [hn[:, kc, offs[g]:offs[g] + n] for kc in range(8)]
                    hr = [("hn", bi % 2, slot[g], c) for c in range(8)]
                    mm_group([(psb[pa][:, 0:n], wbuf[k][:, 0, kc, :], hv[kc], kc == 0, kc == 7) for kc in range(8)],
                             r=[("wbuf", k, 0)] + hr, w=[PS(pa)])
                    mm_group([(psb[pb_][:, 0:n], wbuf[k][:, 1, kc, :], hv[kc], kc == 0, kc == 7) for kc in range(8)],
                             r=[("wbuf", k, 1)] + hr, w=[PS(pb_)])
                    act(sab[s_][:, 0:n], psb[pa][:, 0:n], AF.Silu, w=[PS(pa), ("sa", s_)])
                    tt("dve", hid[:, j, offs[g]:offs[g] + n], sab[s_][:, 0:n], psb[pb_][:, 0:n], ALU.mult,
                       r=[("sa", s_)], w=[PS(pb_), ("hid", slot[g], j)])
                if j in nxt_at:
                    f0_stage(bi + 1, nxt_at[j][0], nxt_at[j][1])
                if j in nxt_ld:
                    f0_load(nxt_ld[j])
                prep_tick(1)
            mark('ffn%d_b%d_F2' % (l, bi))
            for d_ in range(8):
                if d_ + 1 < 8:
                    load_w2(d_ + 1)
                    load_hres(blk, slot, d_ + 1)
                elif nxt:
                    load_w13(0)
                    load_w13(1)
                k = d_ % 2
                for g in blk:
                    t0, n = groups[g]
                    col = 1 if g == 8 else 0
                    o2 = cnt["o"] % 4
                    cnt["o"] += 1
                    po = (5, 6, 1, 2)[o2]
                    hk_ = (d_ % 2) * 3 + slot[g]
                    mm_group([(psb[po][:, 0:n], w2buf[k][:, j, :], hid[:, j, offs[g]:offs[g] + n], j == 0, j == NJ - 1) for j in range(NJ)],
                             r=[("w2buf", k)] + [("hid", slot[g], j) for j in range(NJ)], w=[PS(po)])
                    stt("dve", hres[hk_][:, 0:n], psb[po][:, 0:n], modT[l][:, col, 40 + d_:41 + d_], hres[hk_][:, 0:n], ALU.mult, ALU.add,
                        r=[vk], w=[PS(po), ("hres", hk_)])
                    dma(hT_c[d_, :, t0:t0 + n], hres[hk_][:, 0:n], r=[("hres", hk_)], w=[("hT", cur, g, d_)])
                prep_tick(1)
        P.barrier()
        A.release(mF)

    def final_phase():
        mark('final')
        mF = A.mark()
        yf2 = [A.alloc([8, 512], F32) for _ in range(2)]
        hin2 = A.alloc([8, HW], F32)
        ot = [A.alloc([D], F32) for _ in range(2)]
        hT_c = hT[cur]
        occ = [0]

        def mkf(g):
            t0, n = groups[g]
            hb, hkey_ = (hin, "hin") if g % 2 == 0 else (hin2, "hin2")
            yf = yf2[g % 2]

            def A_():
                dma(hb[:, :, 0:n], hview(hT_c, t0, n), r=hk(cur, g), w=[hkey_])
                norm_mod(n, gfin, None, lambda c: yf[:, c, :], [("yf", g % 2, c) for c in range(8)], hkey=hkey_, hbuf=hb, vkey="gfin")

            def T_():
                for t in range(4):
                    k = occ[0] % 2
                    occ[0] += 1
                    for half in range(2):
                        b = 1 + half
                        def f(e, t=t, half=half, b=b):
                            i = None
                            for c in range(4):
                                i = e.transpose(psb[b][:, c * 128:(c + 1) * 128], yf[:, half * 4 + c, t * 128:(t + 1) * 128], ident_f)
                            return i
                        pe_count[0] += 4
                        P.add("pe", f, r=[("yf", g % 2, half * 4 + c) for c in range(4)] + ["ident_f"], w=[PS(b)])
                        cp("dve" if half == 0 else "act", ot[k][:, half * 512:(half + 1) * 512], psb[b][:, :], w=[PS(b), ("ot", k, half)])
                    dma(out_d[t0 + t * 128:t0 + (t + 1) * 128, :], ot[k], r=[("ot", k, 0), ("ot", k, 1)], w=[("out", g, t)])
            return A_, T_
        fu = [mkf(g) for g in range(8)]
        fu[0][0]()
        for g in range(8):
            if g + 1 < 8:
                fu[g + 1][0]()
            fu[g][1]()
        A.release(mF)


    def pool_phase(l, with_ctx):
        nonlocal cur
        mark('pool%d' % l)
        j = l // 2
        mP = A.mark()
        prep_require(("modv", l, 2))
        vk = ("mod", l, 2)
        vk1 = ("mod", l, 1)
        wp = A.alloc([8, 256], BF16)
        band = A.alloc([4, 5, 128], BF16)
        xnb2 = [A.alloc([8, 512], BF16) for _ in range(2)]
        NH, NZ = 4, 4
        hin4 = [(hin, "hin")] + [(A.alloc([8, HW], F32), "hin%d" % (q_ + 2)) for q_ in range(NH - 1)]
        zt = [A.alloc([4, 1024], BF16) for _ in range(NZ)]
        hnew2 = [A.alloc([8, 512], F32) for _ in range(2)]
        prep_require(("wbpool", j))
        dma(wp, wbpool[j].rearrange("p (a d) -> p a d", a=8), r=[("wbpool", j)], w=["wp"])
        dma(band, band_d, w=["band"])
        src, dst = hT[cur], hT[1 - cur]
        glist = list(range(8)) + ([8] if with_ctx else [])
        zc = [0]
        yc = [0]

        def L_(g):
            t0, n = groups[g]
            hinb, hink = hin4[g % NH]
            dma(hinb[:, :, 0:n], hview(src, t0, n), r=hk(cur, g), w=[hink])

        def A_(g):
            t0, n = groups[g]
            col = 1 if g == 8 else 0
            par = g % 2
            hinb, hink = hin4[g % NH]
            norm_mod(n, gs1[l][:, col, :], modT[l][:, col, 0:8], lambda c: xnb2[par][:, c, 0:n], [("xnb", par, c) for c in range(8)],
                     hkey=hink, hbuf=hinb, vkey=vk1)

        def Z_(g):
            t0, n = groups[g]
            par = g % 2
            for t in range(n // 128):
                bz = ((1, 2), (3, 4))[zc[0] % 2]
                zc[0] += 1
                for hb_ in range(2):
                    items = []
                    for gg in (2 * hb_, 2 * hb_ + 1):
                        for kc in range(2):
                            items.append((psb[bz[hb_]][:, (gg % 2) * 256:(gg % 2 + 1) * 256], xnb2[par][:, 2 * gg + kc, t * 128:(t + 1) * 128],
                                          wp[:, 2 * gg + kc, :], kc == 0, kc == 1))
                    mm_group(items, r=["wp"] + [("xnb", par, c) for c in range(4 * hb_, 4 * hb_ + 4)], w=[PS(bz[hb_])])
                    cp("act" if hb_ == 0 else "dve", zt[g % NZ][:, t, hb_ * 512:(hb_ + 1) * 512], psb[bz[hb_]][:, :], w=[PS(bz[hb_]), ("zt", g % NZ, t, hb_)])
            prep_tick(1)

        def ztile(T):
            gi = T // 4
            return zt[gi % NZ][:, T % 4, :], [("zt", gi % NZ, T % 4, 0), ("zt", gi % NZ, T % 4, 1)]

        def Y_(g):
            t0, n = groups[g]
            col = 1 if g == 8 else 0
            par = g % 2
            hinb, hink = hin4[g % NH]
            hnew = hnew2[par]
            hnk = ("hnew", par)
            s0, s1 = (0, 32) if g < 8 else (32, 34)
            for d_ in range(8):
                wi = d_ // 2
                by = (5, 6, 7)[yc[0] % 3]
                yc[0] += 1
                items = []
                rkeys = ["band"]
                for t in range(n // 128):
                    T = t0 // 128 + t
                    parts = []
                    if T > s0:
                        parts.append((T - 1, 1))
                    parts.append((T, 3 if T == s0 else (4 if T == s1 - 1 else 0)))
                    if T < s1 - 1:
                        parts.append((T + 1, 2))
                    for i_, (Tz, kind) in enumerate(parts):
                        zv, zk = ztile(Tz)
                        rkeys += zk
                        items.append((psb[by][:, t * 128:(t + 1) * 128], zv[:, d_ * 128:(d_ + 1) * 128], band[:, wi, kind, :],
                                      i_ == 0, i_ == len(parts) - 1))
                mm_group(items, r=rkeys, w=[PS(by)])
                stt("dve", hnew[:, d_, 0:n], psb[by][:, 0:n], gp[l][:, col, d_:d_ + 1], hinb[:, d_, 0:n], ALU.mult, ALU.add,
                    r=[hink, vk], w=[PS(by), hnk])
            dma(hview(dst, t0, n), hnew[:, :, 0:n], r=[hnk], w=hk(1 - cur, g))
            prep_tick(1)

        NU = len(glist)
        L_(glist[0])
        for i_ in range(NU + 2):
            if i_ + 1 < NU:
                L_(glist[i_ + 1])
            if i_ < NU:
                A_(glist[i_])
            if 0 <= i_ - 2 < NU:
                Y_(glist[i_ - 2])
            if i_ < NU:
                Z_(glist[i_])
        cur = 1 - cur
        P.barrier()
        A.release(mP)

    def even_phase(l):
        j = l // 2
        upd_ctx = (l == 0)
        mark('E1_%d' % l)
        mE = A.mark()
        NTT = NT // 128
        U = A.alloc([NTT, 512], BF16)
        win = A.alloc([8, INW], BF16)
        mQ = A.mark()
        qT = A.alloc([4, NT], BF16)
        kT = A.alloc([2, NT], BF16)
        V = A.alloc([NTT, 2, 66], BF16)
        mW = A.mark()
        xnb = A.alloc([8, 512], BF16)
        cosb = A.alloc([512], F32)
        sinb = A.alloc([512], F32)
        t1s = [(tbuf[0], ("tb", 0)), (tbuf[1], ("tb", 1))]
        t2s = [(tbuf[2], ("tb", 2)), (A.alloc([512], F32), ("t2x", 0))]
        prep_require(("wbin", j))
        dma(win, wbin[j].rearrange("p (kc n) -> p kc n", kc=8), r=[("wbin", j)], w=["win"])
        P.add("pool", lambda e: e.memset(V[:, :, :, 64:66], 1.0), w=["V1"])
        hT_c = hT[cur]
        rc = 0
        xnbs = [xnb, sq[:, :, 0:512]]

        def e1_load(g):
            t0, n = groups[g]
            dma(hin[:, :, 0:n], hview(hT_c, t0, n), r=hk(cur, g), w=["hin"])

        def e1_norm(g):
            t0, n = groups[g]
            col = 1 if g == 8 else 0
            par = g % 2
            xb = xnbs[par]
            xk = [("xnb", par, c) for c in range(8)]
            norm_mod(n, gs1[l][:, col, :], modT[l][:, col, 0:8], lambda c: xb[:, c, 0:n], xk, vkey=("mod", l, 1), sqbuf=xb, sqkeys=xk)

        def e1_uv(g):
            t0, n = groups[g]
            isctx = g == 8
            par = g % 2
            xb = xnbs[par]
            xr = [("xnb", par, c) for c in range(8)]
            nt_ = n // 128
            if not isctx:
                dma(cosb[:, 0:n], cosT_d[:, t0:t0 + n], w=["cosb"])
                dma(sinb[:, 0:n], sinT_d[:, t0:t0 + n], w=["sinb"])
            for t in range(nt_):
                tile = t0 // 128 + t
                if (not isctx) or upd_ctx:
                    b = 1 + t % 2
                    mm_group([(psb[b][:, :], xb[:, kc, t * 128:(t + 1) * 128], win[:, kc, 0:512], kc == 0, kc == 7) for kc in range(8)],
                             r=xr + ["win"], w=[PS(b)])
                    cp("act", U[:, tile, :], psb[b][:, :], w=[PS(b), ("U", tile)])
                b = (7, 0)[t % 2]
                mm_group([(psb[b][:, 0:128], xb[:, kc, t * 128:(t + 1) * 128], win[:, kc, 2048:2176], kc == 0, kc == 7) for kc in range(8)],
                         r=xr + ["win"], w=[PS(b)])
                cp("dve", V[:, tile, :, 0:64], psb[b][:, 0:128].rearrange("p (g d) -> p g d", g=2), w=[PS(b), ("V", tile)])
            prep_tick(1)

        def e1_qk(g):
            nonlocal rc
            t0, n = groups[g]
            isctx = g == 8
            par = g % 2
            xb = xnbs[par]
            xr = [("xnb", par, c) for c in range(8)]
            specs = []
            if (not isctx) or upd_ctx:
                specs += [("q", qc, 512 + qc * 128, 1024 + qc * 128) for qc in range(4)]
            specs += [("k", kc2, 1536 + kc2 * 128, 1792 + kc2 * 128) for kc2 in range(2)]
            for si_, (kind, ci, c_main, c_sw) in enumerate(specs):
                bm, bs = ((3, 4), (5, 6))[si_ % 2]
                dstv = (qT if kind == "q" else kT)[:, ci, t0:t0 + n]
                dkey = (kind, ci, g)
                mm_group([(psb[bm][:, 0:n], win[:, kc, c_main:c_main + 128], xb[:, kc, 0:n], kc == 0, kc == 7) for kc in range(8)],
                         r=xr + ["win"], w=[PS(bm)])
                if isctx:
                    cp("act", dstv, psb[bm][:, 0:n], w=[PS(bm), dkey])
                    continue
                mm_group([(psb[bs][:, 0:n], win[:, kc, c_sw:c_sw + 128], xb[:, kc, 0:n], kc == 0, kc == 7) for kc in range(8)],
                         r=xr + ["win"], w=[PS(bs)])
                k2 = rc % 2
                rc += 1
                (t1a, t1k), (t2a, t2k) = t1s[k2], t2s[k2]
                tt("dve", t1a[:, 0:n], psb[bm][:, 0:n], cosb[:, 0:n], ALU.mult, r=["cosb"], w=[PS(bm), t1k])
                tt("dve", t2a[:, 0:n], psb[bs][:, 0:n], sinb[:, 0:n], ALU.mult, r=["sinb"], w=[PS(bs), t2k])
                tt("dve", dstv, t1a[:, 0:n], t2a[:, 0:n], ALU.add, r=[t1k, t2k], w=[dkey])
                if ci % 2 == 1:
                    prep_tick(1)

        e1_load(0)
        e1_norm(0)
        for g in range(9):
            if g + 1 < 9:
                e1_load(g + 1)
            e1_uv(g)
            if g + 1 < 9:
                e1_norm(g + 1)
            e1_qk(g)
        P.barrier()
        A.release(mW)
        if os.environ.get("EVSTOP") == "1":
            A.release(mE)
            return
        mark('E2_%d' % l)
        assert 8 * INW == 4 * NT
        YfT = win.rearrange("p a b -> p (a b)").rearrange("p (c t) -> p c t", c=4)
        mD = A.mark()
        NDB = 4
        dbuf = [A.alloc([4, 512], BF16) for _ in range(NDB)]
        pqb = [A.alloc([512], BF16) for _ in range(2)]
        dc_ = 0
        pc_ = 0
        orth = 1.0 / math.sqrt(S * 128.0)
        for kg in range(8):
            for q8 in range(8):
                k = dc_ % NDB
                dc_ += 1
                dma(dbuf[k], dft_d[kg, q8 * 512:(q8 + 1) * 512].rearrange("(t p) a k -> p t (a k)", p=128), w=[("dbuf", k)])
                for hd in range(4):
                    items = [(psb[1 + hd][:, :], U[:, q8 * 4 + t, hd * 128:(hd + 1) * 128], dbuf[k][:, t, :], (q8 == 0 and t == 0), (q8 == 7 and t == 3)) for t in range(4)]
                    mm_group(items, r=[("dbuf", k)] + [("U", q8 * 4 + t) for t in range(4)], w=[PS(1 + hd)])
                if q8 % 2 == 1:
                    prep_tick(1)
            for hd in range(4):
                k2 = pc_ % 2
                pc_ += 1
                cp("act" if hd % 2 == 0 else "dve", pqb[k2], psb[1 + hd][:, :], w=[PS(1 + hd), ("pq", k2)])
                b = 6 + hd % 2
                mm_group([(psb[b][:, 0:256], csc[:, 0, :], pqb[k2][:, 0:256], True, False),
                          (psb[b][:, 0:256], csc[:, 1, :], pqb[k2][:, 256:512], False, True),
                          (psb[b][:, 256:512], csc[:, 0, :], pqb[k2][:, 0:256], True, False),
                          (psb[b][:, 256:512], csc[:, 2, :], pqb[k2][:, 256:512], False, True)], r=[("pq", k2), "csc"], w=[PS(b)])
                act(YfT[:, hd, kg * 256:(kg + 1) * 256], psb[b][:, 0:256], AF.Identity, w=[PS(b), ("Yf", hd, kg)], scale=orth)
                if kg == 0:
                    act(YfT[:, hd, S - 1:S - 256:-1], psb[b][:, 257:512], AF.Identity, w=[PS(b), ("Yf", hd, 15)], scale=orth)
                else:
                    m0_ = S - kg * 256
                    act(YfT[:, hd, m0_:m0_ - 256:-1], psb[b][:, 256:512], AF.Identity, w=[PS(b), ("Yf", hd, 16 - kg), ("Yf", hd, 15 - kg)], scale=orth)
        pn = pqb[0][:, 0:8]
        for hd in range(4):
            mm_group([(psb[0][:, hd:hd + 1], U[:, t, hd * 128:(hd + 1) * 128], altv[:, 0:1], t == 0, t == 31) for t in range(32)],
                     r=[("U", t) for t in range(32)] + ["alt"], w=[PS(0)])
        cp("act", pn[:, 0:4], psb[0][:, 0:4], w=[PS(0), ("pq", 0)])
        for hd in range(4):
            mm_group([(psb[0][:, 8 + hd:9 + hd], csc[:, 0, :], pn[:, hd:hd + 1], True, True)], r=[("pq", 0), "csc"], w=[PS(0)])
        for hd in range(4):
            act(YfT[:, hd, S // 2:S // 2 + 1], psb[0][:, 8 + hd:9 + hd], AF.Identity, w=[PS(0), ("Yf", hd, 8)], scale=orth)
        if upd_ctx:
            orthc = 1.0 / math.sqrt(L * 128.0)
            for hd in range(4):
                k2 = pc_ % 2
                pc_ += 1
                mm_group([(psb[1 + hd][:, :], U[:, 32 + t, hd * 128:(hd + 1) * 128], dft256[:, t, :, :].rearrange("p a k -> p (a k)"), t == 0, t == 1) for t in range(2)],
                         r=[("U", 32), ("U", 33), "dft256"], w=[PS(1 + hd)])
                cp("act" if hd % 2 == 0 else "dve", pqb[k2], psb[1 + hd][:, :], w=[PS(1 + hd), ("pq", k2)])
                b = 6 + hd % 2
                mm_group([(psb[b][:, 0:256], csc[:, 0, :], pqb[k2][:, 0:256], True, False),
                          (psb[b][:, 0:256], csc[:, 1, :], pqb[k2][:, 256:512], False, True)], r=[("pq", k2), "csc"], w=[PS(b)])
                act(YfT[:, hd, S:S + 256], psb[b][:, 0:256], AF.Identity, w=[PS(b), ("Yf", hd, 16)], scale=orthc)
        P.barrier()
        A.release(mD)
        if os.environ.get("EVSTOP") == "2":
            A.release(mE)
            return
        mark('E3_%d' % l)
        attnT = U.rearrange("p t f -> p (t f)")[:, 0:4 * NT].rearrange("p (c t) -> p c t", c=4)
        mA = A.mark()
        NEB = 14
        Eb = [A.alloc([512], BF16) for _ in range(NEB)]
        atok = [A.alloc([512], BF16) for _ in range(2)]
        den = [A.alloc([4], F32) for _ in range(2)]
        cn = {"e": 0, "s": 0, "o": 0, "a": 0}
        qtiles = list(range(32)) + ([32, 33] if upd_ctx else [])
        pt_b = psb[7][:, :].bitcast(BF16)

        def make_unit(qb, g2, ak):
            isctx = qb >= 32
            if isctx:
                kbs = [(32, None), (33, None)]
            else:
                kbs = []
                if qb > 0:
                    kbs.append((qb - 1, maskL))
                kbs.append((qb, None))
                if qb < 31:
                    kbs.append((qb + 1, maskR))
                kbs += [(32, None), (33, None)]
            u = {"ets": [], "kbs": kbs, "qb": qb, "g2": g2, "ak": ak}
            u["po"] = 4 + cn["o"] % 2
            u["dk"] = cn["o"] % 2
            cn["o"] += 1

            def s_step(i_):
                kb, msk = kbs[i_]
                bA, bB = ((0, 1), (2, 3))[cn["s"] % 2]
                cn["s"] += 1
                ek = cn["e"] % NEB
                cn["e"] += 1
                gq = qb // 4 if qb < 32 else 8
                gk = kb // 4 if kb < 32 else 8
                items = []
                for jh in range(4):
                    h_ = 4 * g2 + jh
                    qc = h_ // 2
                    hf = h_ % 2
                    bb = bA if hf == 0 else bB
                    items.append((psb[bb][:, (jh // 2) * 128:(jh // 2 + 1) * 128], kT[hf * 64:(hf + 1) * 64, g2, kb * 128:(kb + 1) * 128],
                                  qT[hf * 64:(hf + 1) * 64, qc, qb * 128:(qb + 1) * 128], True, True))
                mm_group(items, r=[("k", g2, gk)] + [("q", qc_, gq) for qc_ in (2 * g2, 2 * g2 + 1)], w=[PS(bA), PS(bB)])
                Ev = Eb[ek].rearrange("p (a f q) -> p a f q", a=2, f=2)
                sin_ = ps_all[:, bA * 512:(bA + 2) * 512].rearrange("p (f x) -> p f x", f=2)[:, :, 0:256].rearrange("p f (a q) -> p f a q", a=2)
                act(Ev.transpose([0, 2, 1, 3]), sin_, AF.Exp, w=[PS(bA), PS(bB), ("E", ek)], scale=0.125)
                if msk is not None:
                    tt("dve", Eb[ek].rearrange("p (j q) -> p j q", j=4), Eb[ek].rearrange("p (j q) -> p j q", j=4),
                       msk.unsqueeze(1).to_broadcast([128, 4, 128]), ALU.mult, r=["maskL", "maskR"], w=[("E", ek)])
                u["ets"].append((ek, kb))

            def pv_step(jh):
                ets = u["ets"]
                po = u["po"]
                items = [(psb[po][:, jh * 66:(jh + 1) * 66], Eb[ek][:, jh * 128:(jh + 1) * 128], V[:, kb, g2, :], i_ == 0, i_ == len(ets) - 1)
                         for i_, (ek, kb) in enumerate(ets)]
                mm_group(items, r=[("E", ek) for (ek, _) in ets] + [("V", kb) for (_, kb) in ets] + ["V1"], w=[PS(po)])

            def fin():
                po, dk = u["po"], u["dk"]
                ov = psb[po][:, 0:264].rearrange("p (j d) -> p j d", d=66)
                tt("dve", den[dk], ov[:, :, 64], esink[:, j, 4 * g2:4 * g2 + 4], ALU.add, r=["esink"], w=[PS(po), ("den", dk)])
                P.add("dve", lambda e: e.reciprocal(den[dk], den[dk]), w=[("den", dk)])
                tt("dve", atok[ak][:, g2 * 256:(g2 + 1) * 256].rearrange("p (j d) -> p j d", d=64), ov[:, :, 0:64],
                   den[dk].unsqueeze(2).to_broadcast([128, 4, 64]), ALU.mult, r=[("den", dk)], w=[PS(po), ("atok", ak)])
                if g2 == 1:
                    def ftr(e):
                        i = None
                        for c in range(4):
                            i = e.transpose(pt_b[:, c * 128:(c + 1) * 128], atok[ak][:, c * 128:(c + 1) * 128], ident_b)
                        return i
                    pe_count[0] += 4
                    P.add("pe", ftr, r=[("atok", ak), "ident_b"], w=[PS(7)])
                    cp("dve", attnT[:, :, qb * 128:(qb + 1) * 128], pt_b[:, 0:512].rearrange("p (c q) -> p c q", c=4), w=[PS(7), ("attnT", qb)])
                    prep_tick(1, allow_ada=False)
            u["s"], u["pv"], u["fin"] = s_step, pv_step, fin
            return u

        prev = None
        for qb in qtiles:
            ak = cn["a"] % 2
            cn["a"] += 1
            for g2 in range(2):
                u = make_unit(qb, g2, ak)
                pvq = list(range(4)) if prev is not None else []
                for i_ in range(len(u["kbs"])):
                    u["s"](i_)
                    if i_ >= 1 and pvq:
                        prev["pv"](pvq.pop(0))
                while pvq:
                    prev["pv"](pvq.pop(0))
                if prev is not None:
                    prev["fin"]()
                prev = u
        for jh in range(4):
            prev["pv"](jh)
        prev["fin"]()
        P.barrier()
        A.release(mQ)
        if os.environ.get("EVSTOP") == "3":
            A.release(mE)
            return
        mark('E4_%d' % l)
        prep_require(("modv", l, 2))
        vk = ("mod", l, 2)
        wout = A.alloc([8, D], BF16)
        hin2 = A.alloc([8, HW], F32)
        hnew2 = [A.alloc([8, 512], F32) for _ in range(2)]
        prep_require(("wbout", j))
        dma(wout, wbout[j].rearrange("p (kc n) -> p kc n", kc=8), r=[("wbout", j)], w=["wout"])
        glist = list(range(8)) + ([8] if upd_ctx else [])
        for g in glist:
            t0, n = groups[g]
            col = 1 if g == 8 else 0
            hnew = hnew2[g % 2]
            hnk = ("hnew", g % 2)
            hinb, hink = (hin, "hin") if g % 2 == 0 else (hin2, "hin2")
            dma(hinb[:, :, 0:n], hview(hT_c, t0, n), r=hk(cur, g), w=[hink])
            for d_ in range(8):
                b = 1 + d_ % 6
                items = []
                for kc in range(8):
                    rhs = YfT[:, kc, t0:t0 + n] if kc < 4 else attnT[:, kc - 4, t0:t0 + n]
                    items.append((psb[b][:, 0:n], wout[:, kc, d_ * 128:(d_ + 1) * 128], rhs, kc == 0, kc == 7))
                mm_group(items, r=["wout"], w=[PS(b)])
                stt("dve", hnew[:, d_, 0:n], psb[b][:, 0:n], modT[l][:, col, 16 + d_:17 + d_], hinb[:, d_, 0:n], ALU.mult, ALU.add,
                    r=[hink, vk], w=[PS(b), hnk])
                if d_ % 4 == 3:
                    prep_tick(1)
            dma(hview(hT_c, t0, n), hnew[:, :, 0:n], r=[hnk], w=hk(cur, g))
        P.barrier()
        A.release(mE)

    for l in range(nlayers):
        upd = l < 2
        prep_require(("modv", l, 1))
        if enable_mix and l in mix_layers:
            if l % 2 == 0:
                even_phase(l)
            else:
                pool_phase(l, upd)
        if enable_ffn:
            ffn_phase(l, upd)
    final_phase()
    mark('end')
    P.finalize()

    with nc.Block() as block:
        @block.sync
        def _(h):
            P.emit("sp", h, psem, dsem)
            P.final_waits(h, dsem)

        @block.tensor
        def _(h):
            P.emit("pe", h, psem, dsem)

        @block.vector
        def _(h):
            P.emit("dve", h, psem, dsem)

        @block.scalar
        def _(h):
            P.emit("act", h, psem, dsem)

        @block.gpsimd
        def _(h):
            P.emit("pool", h, psem, dsem)
    st.close()
    build.peak = A.peak
    build.marks = marks
    return nc


def make_in_maps(inputs, cores):
    c = host_consts()
    cols = in_ext_cols()
    f32 = lambda a: np.ascontiguousarray(np.asarray(a, dtype=np.float32))
    x = f32(inputs["x"])
    cc = f32(inputs["c"])
    ctx = f32(inputs["ctx"])
    c_ctx = f32(inputs["c_ctx"])
    shared = {
        "ada_w": f32(inputs["ada_w"]),
        "adab": np.ascontiguousarray(f32(inputs["ada_b"]).reshape(DEPTH, 48, 128).transpose(2, 0, 1)),
        "gmix": fm(inputs["norm_mix_g"]),
        "gffn": fm(inputs["norm_ffn_g"]),
        "gfin": fm(inputs["final_g"]),
        "w_in": np.ascontiguousarray(f32(inputs["mix_in_w"])[:, :, cols]),
        "w_out": f32(inputs["mix_out_w"]),
        "sink": np.ascontiguousarray(np.broadcast_to(f32(inputs["attn_sink"])[None], (128, 2, 8))),
        "pool_w": f32(inputs["pool_w"]),
        "pscale": fm(inputs["pool_scale"]),
        "ffn_w1": f32(inputs["ffn_w1"]),
        "ffn_w3": f32(inputs["ffn_w3"]),
        "ffn_w2": f32(inputs["ffn_w2"]),
        "ident_f": c["ident_f"], "ident_b": c["ident_b"], "onesm": c["onesm"], "maskL": c["maskL"], "maskR": c["maskR"],
        "csc": c["csc"], "alt": c["alt"], "dft": c["dft"], "dft256": c["dft256"], "cosT": c["cosT"], "sinT": c["sinT"], "invc": c["invc"], "band": c["band"],
    }
    maps = []
    for b in cores:
        m = dict(shared)
        m["x"] = x[b]
        m["ctx"] = ctx[b]
        cv = np.stack([cc[b], c_ctx], axis=-1)
        m["cvec"] = np.ascontiguousarray(cv.reshape(8, 128, 2).transpose(1, 0, 2))
        maps.append(m)
    return maps


_NC_CACHE = {}


def kernel(**inputs):
    key = "full"
    if key not in _NC_CACHE:
        _NC_CACHE[key] = build()
    nc = _NC_CACHE[key]
    maps = make_in_maps(inputs, list(range(8)))
    res = run_bass_kernel_spmd(nc, maps, core_ids=list(range(8)))
    out = np.stack([np.asarray(r["out"], dtype=np.float32) for r in res.results], axis=0)
    return out
```

```python
import os
import math
import numpy as np
import ml_dtypes
from contextlib import ExitStack
import concourse.bass as bass
import concourse.mybir as mybir
from concourse.bass_utils import run_bass_kernel_spmd

F32 = mybir.dt.float32
BF16 = mybir.dt.bfloat16
AF = mybir.ActivationFunctionType
ALU = mybir.AluOpType

D = 1024
S = 4096
L = 256
NT = S + L
DEPTH = 4
HID = 2816
NJ = HID // 128
EPS = 1e-6
POOL_W = (2, 4, 8, 16)
INW = 2176


class Op:
    __slots__ = ("eng", "fn", "deps", "dma", "needs_inc", "count", "didx", "name")


class Prog:
    ENGS = ("pe", "act", "dve", "pool", "sp")
    NDMA = {"sp": 14, "pool": 6, "act": 4}

    def __init__(self):
        self.streams = {e: [] for e in self.ENGS}
        self.lastw = {}
        self.readers = {}
        self.ndma = {e: 0 for e in self.ENGS}
        self.dmaops = {e: [] for e in self.ENGS}

    def add(self, eng, fn, r=(), w=(), dma=False, name="", extra=()):
        o = Op()
        o.eng, o.fn, o.dma, o.needs_inc, o.name = eng, fn, dma, False, name
        o.count = None
        o.didx = None
        deps = list(extra)
        for k in r:
            x = self.lastw.get(k)
            if x is not None:
                deps.append(x)
        for k in w:
            x = self.lastw.get(k)
            if x is not None:
                deps.append(x)
            deps.extend(self.readers.get(k, ()))
        dd = []
        seen = set()
        for d in deps:
            if id(d) in seen or d is o:
                continue
            seen.add(id(d))
            if (not d.dma) and d.eng == "pe" and eng == "pe" and not dma:
                continue
            dd.append(d)
            d.needs_inc = True
        o.deps = dd
        for k in r:
            self.readers.setdefault(k, []).append(o)
        for k in w:
            self.lastw[k] = o
            self.readers[k] = []
        if dma:
            o.didx = self.ndma[eng]
            self.ndma[eng] += 1
            self.dmaops[eng].append(o)
        self.streams[eng].append(o)
        return o

    def barrier(self):
        deps = []
        for e in self.ENGS:
            for o in reversed(self.streams[e]):
                if not o.dma and o.fn is not None:
                    deps.append(o)
                    break
            K = self.NDMA.get(e, 1)
            deps.extend(self.dmaops[e][-K:])
        for e in self.ENGS:
            self.add(e, None, extra=deps, name="barrier")

    def finalize(self):
        for e in self.ENGS:
            c = 0
            for o in self.streams[e]:
                if o.dma or o.fn is None:
                    continue
                if o.needs_inc:
                    c += 1
                    o.count = c

    def emit(self, eng, h, psem, dsem):
        waited = {}
        K = self.NDMA.get(eng, 1)
        for o in self.streams[eng]:
            for d in o.deps:
                if d.dma:
                    kk = self.NDMA[d.eng]
                    sem = dsem[d.eng][d.didx % kk]
                    val = 16 * (d.didx // kk + 1)
                    key = ("d", d.eng, d.didx % kk)
                else:
                    sem = psem[d.eng]
                    val = d.count
                    key = ("p", d.eng)
                if waited.get(key, 0) >= val:
                    continue
                waited[key] = val
                h.wait_ge(sem, val)
            if o.fn is None:
                continue
            if o.dma:
                if o.didx >= K:
                    sem = dsem[eng][o.didx % K]
                    val = 16 * (o.didx // K)
                    key = ("d", eng, o.didx % K)
                    if waited.get(key, 0) < val:
                        waited[key] = val
                        h.wait_ge(sem, val)
                inst = o.fn(h)
                inst.then_inc(dsem[eng][o.didx % K], 16)
            else:
                inst = o.fn(h)
                if o.needs_inc:
                    inst.then_inc(psem[eng], 1)

    def final_waits(self, h, dsem):
        for e in self.ENGS:
            n = self.ndma[e]
            if n == 0:
                continue
            K = self.NDMA[e]
            for s in range(min(K, n)):
                last = ((n - 1 - s) // K) * K + s
                h.wait_ge(dsem[e][s], 16 * (last // K + 1))


class Arena:
    def __init__(self, ap, nbytes):
        self.ap = ap
        self.n = nbytes
        self.top = 0
        self.peak = 0

    def alloc(self, free_shape, dt):
        esz = 4 if dt == F32 else 2
        cnt = 1
        for d in free_shape:
            cnt *= d
        nb = (cnt * esz + 63) // 64 * 64
        off = self.top
        if off + nb > self.n:
            raise RuntimeError("arena overflow: need %d have %d" % (off + nb, self.n))
        self.top += nb
        self.peak = max(self.peak, self.top)
        v = self.ap[:, off // 4:(off + nb) // 4]
        if dt != F32:
            v = v.bitcast(dt)
        v = v[:, 0:cnt]
        if len(free_shape) == 2:
            v = v.rearrange("p (a b) -> p a b", a=free_shape[0])
        elif len(free_shape) == 3:
            v = v.rearrange("p (a b c) -> p a b c", a=free_shape[0], b=free_shape[1])
        return v

    def mark(self):
        return self.top

    def release(self, m):
        self.top = m


_CONST = {}


def _bf(a):
    return np.ascontiguousarray(a.astype(ml_dtypes.bfloat16))


def host_consts():
    if _CONST:
        return _CONST
    c = {}
    c["ident_f"] = np.eye(128, dtype=np.float32)
    c["ident_b"] = _bf(np.eye(128, dtype=np.float32))
    c["onesm"] = _bf(np.full((128, 128), 1.0 / 1024, dtype=np.float32))
    kk = np.arange(128)[:, None]
    qq = np.arange(128)[None, :]
    c["maskL"] = _bf((qq <= kk).astype(np.float32))
    c["maskR"] = _bf((kk <= qq).astype(np.float32))
    ang = 2 * np.pi * (np.arange(128)[:, None] * np.arange(128)[None, :] % 128) / 128.0
    c["csc"] = _bf(np.stack([np.cos(ang), -np.sin(ang), np.sin(ang)], axis=1).astype(np.float32))
    alt = np.zeros((128, 2), dtype=np.float32)
    alt[:, 0] = np.where(np.arange(128) % 2 == 0, 1.0, -1.0)
    c["alt"] = _bf(alt)
    n = np.arange(S, dtype=np.int64)[:, None]
    dft = np.empty((16, S, 2, 256), dtype=ml_dtypes.bfloat16)
    for kg in range(16):
        k = (np.arange(256, dtype=np.int64) + kg * 256)[None, :]
        a = 2 * np.pi * ((n * k) % S).astype(np.float64) / S
        dft[kg, :, 0, :] = np.cos(a).astype(np.float32).astype(ml_dtypes.bfloat16)
        dft[kg, :, 1, :] = np.sin(a).astype(np.float32).astype(ml_dtypes.bfloat16)
    c["dft"] = dft
    n = np.arange(L, dtype=np.int64)[:, None]
    k = np.arange(L, dtype=np.int64)[None, :]
    a = 2 * np.pi * ((n * k) % L).astype(np.float64) / L
    d2 = np.stack([np.cos(a), np.sin(a)], axis=1).astype(np.float32)
    c["dft256"] = _bf(d2.reshape(2, 128, 2, 256).transpose(1, 0, 2, 3))
    rows = S // 64
    row = np.repeat(np.arange(rows, dtype=np.float32), 64)
    col = np.tile(np.arange(64, dtype=np.float32), rows)
    inv = (10000.0 ** (-np.arange(16, dtype=np.float32) / 16)).astype(np.float32)
    angr = np.concatenate([row[:, None] * inv[None], col[:, None] * inv[None]], axis=-1)
    cos = np.cos(angr).astype(np.float32)
    sin = np.sin(angr).astype(np.float32)
    p = np.arange(128)
    pair = (p % 64) // 2
    sign = np.where(p % 2 == 0, -1.0, 1.0).astype(np.float32)
    c["cosT"] = np.ascontiguousarray(cos[:, pair].T)
    c["sinT"] = np.ascontiguousarray((sin[:, pair] * sign[None, :]).T)
    invc = np.ones((4, 2, 8), dtype=np.float32)
    for wi, w in enumerate(POOL_W):
        for t in range(w // 2):
            invc[wi, 0, t] = 1.0 / (t + w // 2)
            invc[wi, 1, t] = 1.0 / (w - t)
    c["invc"] = np.ascontiguousarray(np.broadcast_to(invc[None], (128, 4, 2, 8)))
    band = np.zeros((128, 4, 5, 128), dtype=np.float32)
    Nq = 512
    tq = np.arange(Nq)
    for wi, w in enumerate(POOL_W):
        M = np.zeros((Nq, Nq), dtype=np.float64)
        for tp in range(Nq):
            lo = max(tp - w // 2, 0)
            hi = min(tp + w // 2, Nq)
            M[lo:hi, tp] = 1.0 / (hi - lo)
            M[tp, tp] -= 1.0
        band[:, wi, 0, :] = M[128:256, 128:256]
        band[:, wi, 1, :] = M[0:128, 128:256]
        band[:, wi, 2, :] = M[256:384, 128:256]
        band[:, wi, 3, :] = M[0:128, 0:128]
        band[:, wi, 4, :] = M[384:512, 384:512]
    c["band"] = _bf(band)
    _CONST.update(c)
    return c


def in_ext_cols():
    f = list(range(0, 512))
    q = list(range(512, 1024))
    sw = lambda c: c + 1 if c % 2 == 0 else c - 1
    qs = [sw(c) for c in q]
    k0 = list(range(1024, 1088))
    k1 = list(range(1088, 1152))
    k0s = [sw(c) for c in k0]
    k1s = [sw(c) for c in k1]
    v = list(range(1152, 1280))
    cols = f + q + qs + k0 + k0 + k1 + k1 + k0s + k0s + k1s + k1s + v
    assert len(cols) == INW
    return np.array(cols)


def fm(vec):
    a = np.asarray(vec, dtype=np.float32)
    lead = a.shape[:-1]
    a = a.reshape(lead + (8, 128))
    a = np.moveaxis(a, -1, 0)
    return np.ascontiguousarray(a)


def build(nlayers=DEPTH, enable_mix=True, enable_ffn=True, mix_layers=(0, 1, 2, 3)):
    nc = bass.Bass("TRN2", target_bir_lowering=False)

    def din(name, shape, dt=F32):
        return nc.dram_tensor(name, list(shape), dt, kind="ExternalInput").ap()

    def dscr(name, shape, dt):
        return nc.dram_tensor(name, list(shape), dt, kind="Internal").ap()

    x_d = din("x", [S, D])
    ctx_d = din("ctx", [L, D])
    cvec_d = din("cvec", [128, 8, 2])
    adaw_d = din("ada_w", [DEPTH, D, 6 * D])
    adab_d = din("adab", [128, DEPTH, 48])
    gmix_d = din("gmix", [128, DEPTH, 8])
    gffn_d = din("gffn", [128, DEPTH, 8])
    gfin_d = din("gfin", [128, 8])
    win_d = din("w_in", [2, D, INW])
    wout_d = din("w_out", [2, D, D])
    sink_d = din("sink", [128, 2, 8])
    poolw_d = din("pool_w", [2, 4, 256, 256])
    pscale_d = din("pscale", [128, 2, 8])
    w1_d = din("ffn_w1", [DEPTH, D, HID])
    w3_d = din("ffn_w3", [DEPTH, D, HID])
    w2_d = din("ffn_w2", [DEPTH, HID, D])
    identf_d = din("ident_f", [128, 128])
    identb_d = din("ident_b", [128, 128], BF16)
    onesm_d = din("onesm", [128, 128], BF16)
    maskL_d = din("maskL", [128, 128], BF16)
    maskR_d = din("maskR", [128, 128], BF16)
    csc_d = din("csc", [128, 3, 128], BF16)
    alt_d = din("alt", [128, 2], BF16)
    dft_d = din("dft", [16, S, 2, 256], BF16)
    dft256_d = din("dft256", [128, 2, 2, 256], BF16)
    cosT_d = din("cosT", [128, S])
    sinT_d = din("sinT", [128, S])
    invc_d = din("invc", [128, 4, 2, 8])
    band_d = din("band", [128, 4, 5, 128], BF16)
    out_d = nc.dram_tensor("out", [S, D], F32, kind="ExternalOutput").ap()

    hT = [dscr("hT0", [8, 128, NT], F32), dscr("hT1", [8, 128, NT], F32)]
    wb1 = dscr("wb1", [DEPTH, NJ, 128, 1024], BF16)
    wb3 = dscr("wb3", [DEPTH, NJ, 128, 1024], BF16)
    wb2 = dscr("wb2", [DEPTH, 8, 128, HID], BF16)
    wbin = dscr("wbin", [2, 128, 8 * INW], BF16)
    wbout = dscr("wbout", [2, 128, 8 * D], BF16)
    wbpool = dscr("wbpool", [2, 128, 8 * 256], BF16)

    P = Prog()
    st = ExitStack()
    ARENA_F32 = 53000
    arena_t = st.enter_context(nc.sbuf_tensor("arena", [128, ARENA_F32], F32))
    A = Arena(arena_t[:], ARENA_F32 * 4)
    ps_all = st.enter_context(nc.psum_tensor("ps_all", [128, 8 * 512], F32))
    psb = [ps_all[:, i * 512:(i + 1) * 512] for i in range(8)]
    psem = {e: st.enter_context(nc.semaphore("ps_" + e)) for e in Prog.ENGS}
    dsem = {e: [st.enter_context(nc.semaphore("ds_%s%d" % (e, i))) for i in range(k)] for e, k in Prog.NDMA.items()}

    def PS(b):
        return ("ps", b)

    def dma(out, in_, r=(), w=(), q="sp"):
        return P.add(q, lambda e: e.dma_start(out=out, in_=in_), r=r, w=w, dma=True)

    def act(out, in_, func, r=(), w=(), bias=None, scale=None):
        kw = {}
        if bias is not None:
            kw["bias"] = bias
        if scale is not None:
            kw["scale"] = scale
        return P.add("act", lambda e: e.activation(out=out, in_=in_, func=func, **kw), r=r, w=w)

    def tt(eng, out, in0, in1, op, r=(), w=()):
        return P.add(eng, lambda e: e.tensor_tensor(out, in0, in1, op), r=r, w=w)

    def stt(eng, out, in0, scalar, in1, op0, op1, r=(), w=()):
        return P.add(eng, lambda e: e.scalar_tensor_tensor(out, in0, scalar, in1, op0, op1), r=r, w=w)

    def ts(eng, out, in0, s1, s2, op0, op1=None, r=(), w=()):
        if op1 is None:
            return P.add(eng, lambda e: e.tensor_scalar(out, in0, s1, None, op0), r=r, w=w)
        return P.add(eng, lambda e: e.tensor_scalar(out, in0, s1, s2, op0, op1), r=r, w=w)

    def cp(eng, out, in_, r=(), w=()):
        if eng == "act":
            return P.add("act", lambda e: e.copy(out, in_), r=r, w=w)
        return P.add(eng, lambda e: e.tensor_copy(out, in_), r=r, w=w)

    marks = []
    pe_count = [0]

    def mark(name):
        marks.append((name, pe_count[0]))

    def mm_group(items, r=(), w=()):
        pe_count[0] += len(items)
        def f(e):
            i = None
            for (o, l, rr, s0, s1) in items:
                i = e.matmul(o, l, rr, start=s0, stop=s1)
            return i
        return P.add("pe", f, r=r, w=w)

    ident_f = A.alloc([128], F32)
    ident_b = A.alloc([128], BF16)
    onesm = A.alloc([128], BF16)
    maskL = A.alloc([128], BF16)
    maskR = A.alloc([128], BF16)
    csc = A.alloc([3, 128], BF16)
    altv = A.alloc([2], BF16)
    dft256 = A.alloc([2, 2, 256], BF16)
    cvec = A.alloc([8, 2], F32)
    scb = A.alloc([8, 2], BF16)
    adab = A.alloc([DEPTH, 48], F32)
    gmix = A.alloc([DEPTH, 8], F32)
    gffn = A.alloc([DEPTH, 8], F32)
    gfin = A.alloc([8], F32)
    sink = A.alloc([2, 8], F32)
    esink = A.alloc([2, 8], F32)
    pscale = A.alloc([2, 8], F32)
    invc = A.alloc([4, 2, 8], F32)
    modT = [A.alloc([2, 48], F32) for _ in range(DEPTH)]
    gs1 = [A.alloc([2, 8], F32) for _ in range(DEPTH)]
    gs2 = [A.alloc([2, 8], F32) for _ in range(DEPTH)]
    gp = [A.alloc([2, 8], F32) for _ in range(DEPTH)]
    HW = 528
    hin = A.alloc([8, HW], F32)
    sq = A.alloc([8, HW], BF16)
    rsb = [A.alloc([HW], F32) for _ in range(2)]
    tbuf = [A.alloc([HW], F32) for _ in range(3)]
    PREP_N = 1408
    stg_f = [A.alloc([PREP_N], F32) for _ in range(2)]
    stg_b = [A.alloc([PREP_N], BF16) for _ in range(2)]

    for (sbt, drt, key) in [(ident_f, identf_d, "ident_f"), (ident_b, identb_d, "ident_b"), (onesm, onesm_d, "onesm"),
                            (maskL, maskL_d, "maskL"), (maskR, maskR_d, "maskR"), (csc, csc_d, "csc"), (altv, alt_d, "alt"),
                            (dft256, dft256_d, "dft256"), (cvec, cvec_d, "cvec"), (adab, adab_d, "adab"),
                            (gmix, gmix_d, "gmix"), (gffn, gffn_d, "gffn"), (gfin, gfin_d, "gfin"),
                            (sink, sink_d, "sink"), (pscale, pscale_d, "pscale"), (invc, invc_d, "invc")]:
        dma(sbt, drt, w=[key])
    act(esink, sink, AF.Exp, r=["sink"], w=["esink"])

    prep_q = []
    prep_state = {"i": 0, "n": 0, "pending": None}

    def prep_slab(src_ap, dst_ap, shape, key):
        cnt = 1
        for d_ in shape:
            cnt *= d_

        def A_(k):
            sf = stg_f[k][:, 0:cnt]
            sbv = stg_b[k][:, 0:cnt]
            sf3 = sf.rearrange("p (a b) -> p a b", a=shape[0])
            dma(sf3, src_ap, w=[("stgf", k)])
            cp("act", sbv, sf, r=[("stgf", k)], w=[("stgb", k)])

        def B_(k):
            sb3 = stg_b[k][:, 0:cnt].rearrange("p (a b) -> p a b", a=shape[0])
            dma(dst_ap, sb3, r=[("stgb", k)], w=[key])
        prep_q.append(["slab", key, A_, B_])

    def queue_ada(l, part):
        awv = adaw_d[l].rearrange("(kc p) n -> p kc n", p=128)
        rng_ = range(0, 16) if part == 1 else range(16, 48)
        for pc in rng_:
            def A_(k, pc=pc):
                sf = stg_f[k][:, 0:1024]
                dma(sf.rearrange("p (a b) -> p a b", a=8), awv[:, :, pc * 128:(pc + 1) * 128], w=[("stgf", k)])
                cp("act", stg_b[k][:, 0:1024], sf, r=[("stgf", k)], w=[("stgb", k)])

            def B_(k, pc=pc):
                wv = stg_b[k][:, 0:1024].rearrange("p (a b) -> p a b", a=8)
                mm_group([(psb[7][:, 0:2], wv[:, kc, :], scb[:, kc, :], kc == 0, kc == 7) for kc in range(8)],
                         r=[("stgb", k), "scb"], w=[PS(7)])
                tt("dve", modT[l][:, :, pc], psb[7][:, 0:2], adab[:, l, pc:pc + 1].to_broadcast([128, 2]), ALU.add,
                   r=["adab"], w=[PS(7), ("mod", l, part)])
            prep_q.append(["ada", ("modp", l, part), A_, B_])

        def Afin(k):
            pass

        def Bfin(k):
            for col in range(2):
                if part == 1:
                    stt("dve", gs1[l][:, col, :], modT[l][:, col, 8:16], 1.0, gmix[:, l, :], ALU.add, ALU.mult, r=["gmix"], w=[("mod", l, part)])
                else:
                    stt("dve", gs2[l][:, col, :], modT[l][:, col, 32:40], 1.0, gffn[:, l, :], ALU.add, ALU.mult, r=["gffn"], w=[("mod", l, part)])
                    if l % 2 == 1:
                        tt("dve", gp[l][:, col, :], modT[l][:, col, 16:24], pscale[:, l // 2, :], ALU.mult, r=["pscale"], w=[("mod", l, part)])
        prep_q.append(["adafin", ("modv", l, part), Afin, Bfin])

    def queue_prep_layer_mix(l):
        j = l // 2
        if l % 2 == 0:
            wv = win_d[j].rearrange("(kc p) n -> p kc n", p=128)
            dv = wbin[j].rearrange("p (kc n) -> p kc n", kc=8)
            for s_ in range(INW // 128):
                prep_slab(wv[:, :, s_ * 128:(s_ + 1) * 128], dv[:, :, s_ * 128:(s_ + 1) * 128], [8, 128], ("wbin", j))
            wv = wout_d[j].rearrange("(kc p) n -> p kc n", p=128)
            dv = wbout[j].rearrange("p (kc n) -> p kc n", kc=8)
            for s_ in range(8):
                prep_slab(wv[:, :, s_ * 128:(s_ + 1) * 128], dv[:, :, s_ * 128:(s_ + 1) * 128], [8, 128], ("wbout", j))
        else:
            wv = poolw_d[j].rearrange("g (kc p) d -> p (g kc) d", p=128)
            dv = wbpool[j].rearrange("p (a d) -> p a d", a=8)
            for s_ in range(2):
                prep_slab(wv[:, s_ * 4:(s_ + 1) * 4, :], dv[:, s_ * 4:(s_ + 1) * 4, :], [4, 256], ("wbpool", j))

    def queue_prep_layer_ffn(l):
        w1v = w1_d[l].rearrange("(kc p) n -> p kc n", p=128)
        w3v = w3_d[l].rearrange("(kc p) n -> p kc n", p=128)
        for j in range(NJ):
            prep_slab(w1v[:, :, j * 128:(j + 1) * 128], wb1[l, j].rearrange("p (kc n) -> p kc n", kc=8), [8, 128], ("wb1", l, j))
            prep_slab(w3v[:, :, j * 128:(j + 1) * 128], wb3[l, j].rearrange("p (kc n) -> p kc n", kc=8), [8, 128], ("wb3", l, j))
        w2v = w2_d[l].rearrange("(jc p) n -> p jc n", p=128)
        for d_ in range(8):
            dv = wb2[l, d_].rearrange("p (jc n) -> p jc n", jc=NJ)
            for hf in range(2):
                prep_slab(w2v[:, hf * 11:(hf + 1) * 11, d_ * 128:(d_ + 1) * 128], dv[:, hf * 11:(hf + 1) * 11, :], [11, 128], ("wb2", l, d_, hf))

    prep_state["nslots"] = 2
    prep_state["lag"] = 1
    prep_state["pend"] = []

    def prep_flush():
        while prep_state["pend"]:
            t_, k_ = prep_state["pend"].pop(0)
            t_[3](k_)

    def prep_tick(n=1, allow_ada=True):
        for _ in range(n):
            i_ = prep_state["i"]
            nxt = prep_q[i_] if i_ < len(prep_q) else None
            if nxt is not None and nxt[0] in ("ada", "adafin") and not allow_ada:
                nxt = None
            if nxt is not None:
                k = prep_state["n"] % prep_state["nslots"]
                prep_state["n"] += 1
                nxt[2](k)
                prep_state["pend"].append((nxt, k))
                prep_state["i"] += 1
                while len(prep_state["pend"]) > prep_state["lag"]:
                    t_, k_ = prep_state["pend"].pop(0)
                    t_[3](k_)
            elif prep_state["pend"]:
                t_, k_ = prep_state["pend"].pop(0)
                t_[3](k_)

    def prep_require(key):
        last = -1
        for i_, t_ in enumerate(prep_q):
            if t_[1] == key:
                last = i_
        while prep_state["i"] <= last:
            prep_tick()
        idxs = [q_ for q_, (t_, _) in enumerate(prep_state["pend"]) if t_[1] == key]
        if idxs:
            for _ in range(idxs[-1] + 1):
                t_, k_ = prep_state["pend"].pop(0)
                t_[3](k_)

    act(scb, cvec, AF.Silu, r=["cvec"], w=["scb"])
    for l in range(nlayers):
        queue_ada(l, 1)
        if enable_mix and l in mix_layers:
            queue_prep_layer_mix(l)
        queue_ada(l, 2)
        if enable_ffn:
            queue_prep_layer_ffn(l)

    def norm_s1(n, hkey="hin", col0=0, hbuf=None, sqbuf=None, sqkeys=None):
        if hbuf is None:
            hbuf = hin
        hv = hbuf[:, :, col0:col0 + n]
        if sqbuf is None:
            sqb, sqk = sq, ["sq"]
        else:
            sqb, sqk = sqbuf, sqkeys
        sqv = sqb[:, :, col0:col0 + n]
        act(sqv, hv, AF.Square, r=[hkey], w=sqk)
        mm_group([(psb[0][:, 0:n], onesm, sqb[:, c, col0:col0 + n], c == 0, c == 7) for c in range(8)],
                 r=sqk + ["onesm"], w=[PS(0)])
        k = norm_mod.cnt % 2
        norm_mod.cnt += 1
        rs = rsb[k][:, 0:n]
        act(rs, psb[0][:, 0:n], AF.Sqrt, w=[PS(0), ("rs", k)], bias=EPS, scale=1.0)
        P.add("dve", lambda e: e.reciprocal(rs, rs), w=[("rs", k)])
        return rs, k

    def norm_s2(n, gsv, shv, out_fn, out_keys, rs, k, chunks, hkey="hin", col0=0, hbuf=None, vkey=None):
        if hbuf is None:
            hbuf = hin
        vr = [vkey] if vkey is not None else []
        for c in chunks:
            tk = norm_mod.tc % 3
            norm_mod.tc += 1
            tv = tbuf[tk][:, 0:n]
            stt("dve", tv, hbuf[:, c, col0:col0 + n], gsv[:, c:c + 1], rs, ALU.mult, ALU.mult,
                r=[hkey, ("rs", k)] + vr, w=[("tb", tk)])
            o = out_fn(c)
            ok_ = out_keys[c] if isinstance(out_keys, list) else out_keys
            if shv is None:
                cp("act", o, tv, r=[("tb", tk)], w=[ok_])
            else:
                act(o, tv, AF.Identity, r=[("tb", tk)] + vr, w=[ok_], bias=shv[:, c:c + 1], scale=1.0)

    def norm_mod(n, gsv, shv, out_fn, out_keys, hkey="hin", col0=0, hbuf=None, vkey=None, sqbuf=None, sqkeys=None):
        rs, k = norm_s1(n, hkey=hkey, col0=col0, hbuf=hbuf, sqbuf=sqbuf, sqkeys=sqkeys)
        norm_s2(n, gsv, shv, out_fn, out_keys, rs, k, range(8), hkey=hkey, col0=col0, hbuf=hbuf, vkey=vkey)
    norm_mod.cnt = 0
    norm_mod.tc = 0

    def hview(buf, t0, n):
        return buf.rearrange("c p t -> p c t")[:, :, t0:t0 + n]

    def hk(b, g):
        return [("hT", b, g, d_) for d_ in range(8)]

    groups = [(g * 512, 512) for g in range(8)] + [(S, L)]

    mark('ada')
    m0 = A.mark()
    xt = [A.alloc([4, D], F32) for _ in range(2)]
    hin2p = A.alloc([8, HW], F32)
    for _ in range(4):
        stg_f.append(A.alloc([PREP_N], F32))
        stg_b.append(A.alloc([PREP_N], BF16))
    prep_state["nslots"] = 6
    prep_state["lag"] = 4
    def p0_load(gi):
        t0, n = groups[gi]
        k = gi % 2
        src = x_d[t0:t0 + n, :] if t0 < S else ctx_d[:, :]
        dma(xt[k][:, 0:n // 128, :], src.rearrange("(t p) f -> p t f", p=128), w=[("xt", k)])

    p0_load(0)
    p0c = 0
    for gi, (t0, n) in enumerate(groups):
        k = gi % 2
        hb, hkey_ = (hin, "hin") if k == 0 else (hin2p, "hin2")
        nt_ = n // 128
        if gi + 1 < len(groups):
            p0_load(gi + 1)
        for t in range(nt_):
            for half in range(2):
                b = 1 + p0c % 4
                p0c += 1
                def f(e, k=k, t=t, half=half, b=b):
                    i = None
                    for c in range(4):
                        i = e.transpose(psb[b][:, c * 128:(c + 1) * 128], xt[k][:, t, (half * 4 + c) * 128:(half * 4 + c + 1) * 128], ident_f)
                    return i
                pe_count[0] += 4
                P.add("pe", f, r=[("xt", k), "ident_f"], w=[PS(b)])
                cp("dve" if half == 0 else "act", hb[:, half * 4:(half + 1) * 4, t * 128:(t + 1) * 128],
                   psb[b][:, :].rearrange("p (c n) -> p c n", c=4), w=[PS(b), hkey_])
            prep_tick(1)
        dma(hview(hT[0], t0, n), hb[:, :, 0:n], r=[hkey_], w=hk(0, gi))
        prep_tick(1)
    prep_require(("modv", 0, 1))
    if enable_mix and 0 in mix_layers:
        prep_require(("wbin", 0))
    prep_flush()
    P.barrier()
    A.release(m0)
    del stg_f[2:]
    del stg_b[2:]
    prep_state["nslots"] = 2
    prep_state["lag"] = 1
    prep_state["n"] = 0


    cur = 0

    def ffn_phase(l, with_ctx):
        mark('ffn%d' % l)
        mF = A.mark()
        prep_require(("modv", l, 2))
        vk = ("mod", l, 2)
        blocks = [[0, 1], [2, 3], [4, 5], [6, 7] + ([8] if with_ctx else [])]
        maxtok = max(sum(groups[g][1] for g in b) for b in blocks)
        hn2 = [A.alloc([8, maxtok], BF16) for _ in range(2)]
        hid = A.alloc([NJ, maxtok], BF16)
        wbuf = [A.alloc([2, 8, 128], BF16) for _ in range(3)]
        w2buf = [A.alloc([NJ, 128], BF16) for _ in range(2)]
        sab = [A.alloc([512], F32) for _ in range(3)]
        hres = [A.alloc([512], F32) for _ in range(6)]
        hT_c = hT[cur]
        cnt = {"s": 0, "o": 0}

        def offs_of(blk):
            offs, slot, o_ = {}, {}, 0
            for si, g in enumerate(blk):
                offs[g] = o_
                slot[g] = si
                o_ += groups[g][1]
            return offs, slot

        def f0_load(g):
            t0, n = groups[g]
            dma(hin[:, :, 0:n], hview(hT_c, t0, n), r=hk(cur, g), w=["hin"])

        f0_state = {}

        def f0_stage(bi, g, stage):
            blk = blocks[bi]
            offs, slot = offs_of(blk)
            hn = hn2[bi % 2]
            t0, n = groups[g]
            col = 1 if g == 8 else 0
            if stage == 0:
                f0_state[g] = norm_s1(n)
                return
            rs, k = f0_state[g]
            norm_s2(n, gs2[l][:, col, :], modT[l][:, col, 24:32],
                    lambda c: hn[:, c, offs[g]:offs[g] + n], [("hn", bi % 2, slot[g], c) for c in range(8)], rs, k,
                    range(0, 4) if stage == 1 else range(4, 8), vkey=vk)

        def f0_group(bi, g, load=True):
            blk = blocks[bi]
            offs, slot = offs_of(blk)
            hn = hn2[bi % 2]
            t0, n = groups[g]
            col = 1 if g == 8 else 0
            if load:
                f0_load(g)
            norm_mod(n, gs2[l][:, col, :], modT[l][:, col, 24:32],
                     lambda c: hn[:, c, offs[g]:offs[g] + n], [("hn", bi % 2, slot[g], c) for c in range(8)], vkey=vk)

        def load_w13(j):
            prep_require(("wb1", l, j))
            prep_require(("wb3", l, j))
            k = j % 3
            dma(wbuf[k][:, 0, :, :], wb1[l, j].rearrange("p (kc n) -> p kc n", kc=8), r=[("wb1", l, j)], w=[("wbuf", k, 0)])
            dma(wbuf[k][:, 1, :, :], wb3[l, j].rearrange("p (kc n) -> p kc n", kc=8), r=[("wb3", l, j)], w=[("wbuf", k, 1)])

        def load_w2(d_):
            prep_require(("wb2", l, d_, 0))
            prep_require(("wb2", l, d_, 1))
            k = d_ % 2
            dma(w2buf[k], wb2[l, d_].rearrange("p (jc n) -> p jc n", jc=NJ), r=[("wb2", l, d_, 0), ("wb2", l, d_, 1)], w=[("w2buf", k)])

        def load_hres(blk, slot, d_):
            for g in blk:
                t0, n = groups[g]
                hk_ = (d_ % 2) * 3 + slot[g]
                dma(hres[hk_][:, 0:n], hT_c[d_, :, t0:t0 + n], r=[("hT", cur, g, d_)], w=[("hres", hk_)])

        mark('ffn%d_b0_F0' % l)
        for g in blocks[0]:
            f0_group(0, g)
        load_w13(0)
        load_w13(1)
        for bi, blk in enumerate(blocks):
            offs, slot = offs_of(blk)
            hn = hn2[bi % 2]
            nxt = blocks[bi + 1] if bi + 1 < len(blocks) else []
            nxt_at = {}
            nxt_ld = {}
            for i_, g in enumerate(nxt):
                for st_ in range(3):
                    nxt_at[4 + 5 * i_ + st_] = (g, st_)
                nxt_ld[2 + 5 * i_] = g
            mark('ffn%d_b%d_F1' % (l, bi))
            for j in range(NJ):
                if j + 2 < NJ:
                    load_w13(j + 2)
                elif j + 2 == NJ:
                    load_w2(0)
                    load_hres(blk, slot, 0)
                k = j % 3
                for g in blk:
                    t0, n = groups[g]
                    s_ = cnt["s"] % 3
                    cnt["s"] += 1
                    pa, pb_ = ((1, 3), (2, 4), (5, 6))[s_]
                    hv = [hn[:, kc, offs[g]:offs[g] + n] for kc in range(8)]
                    hr = [("hn", bi % 2, slot[g], c) for c in range(8)]
                    mm_group([(psb[pa][:, 0:n], wbuf[k][:, 0, kc, :], hv[kc], kc == 0, kc == 7) for kc in range(8)],
                             r=[("wbuf", k, 0)] + hr, w=[PS(pa)])
                    mm_group([(psb[pb_][:, 0:n], wbuf[k][:, 1, kc, :], hv[kc], kc == 0, kc == 7) for kc in range(8)],
                             r=[("wbuf", k, 1)] + hr, w=[PS(pb_)])
                    act(sab[s_][:, 0:n], psb[pa][:, 0:n], AF.Silu, w=[PS(pa), ("sa", s_)])
                    tt("dve", hid[:, j, offs[g]:offs[g] + n], sab[s_][:, 0:n], psb[pb_][:, 0:n], ALU.mult,
                       r=[("sa", s_)], w=[PS(pb_), ("hid", slot[g], j)])
                if j in nxt_at:
                    f0_stage(bi + 1, nxt_at[j][0], nxt_at[j][1])
                if j in nxt_ld:
                    f0_load(nxt_ld[j])
                prep_tick(1)
            mark('ffn%d_b%d_F2' % (l, bi))
            for d_ in range(8):
                if d_ + 1 < 8:
                    load_w2(d_ + 1)
                    load_hres(blk, slot, d_ + 1)
                elif nxt:
                    load_w13(0)
                    load_w13(1)
                k = d_ % 2
                for g in blk:
                    t0, n = groups[g]
                    col = 1 if g == 8 else 0
                    o2 = cnt["o"] % 4
                    cnt["o"] += 1
                    po = (5, 6, 1, 2)[o2]
                    hk_ = (d_ % 2) * 3 + slot[g]
                    mm_group([(psb[po][:, 0:n], w2buf[k][:, j, :], hid[:, j, offs[g]:offs[g] + n], j == 0, j == NJ - 1) for j in range(NJ)],
                             r=[("w2buf", k)] + [("hid", slot[g], j) for j in range(NJ)], w=[PS(po)])
                    stt("dve", hres[hk_][:, 0:n], psb[po][:, 0:n], modT[l][:, col, 40 + d_:41 + d_], hres[hk_][:, 0:n], ALU.mult, ALU.add,
                        r=[vk], w=[PS(po), ("hres", hk_)])
                    dma(hT_c[d_, :, t0:t0 + n], hres[hk_][:, 0:n], r=[("hres", hk_)], w=[("hT", cur, g, d_)])
                prep_tick(1)
        P.barrier()
        A.release(mF)

    def final_phase():
        mark('final')
        mF = A.mark()
        yf2 = [A.alloc([8, 512], F32) for _ in range(2)]
        hin2 = A.alloc([8, HW], F32)
        ot = [A.alloc([D], F32) for _ in range(2)]
        hT_c = hT[cur]
        occ = [0]

        def mkf(g):
            t0, n = groups[g]
            hb, hkey_ = (hin, "hin") if g % 2 == 0 else (hin2, "hin2")
            yf = yf2[g % 2]

            def A_():
                dma(hb[:, :, 0:n], hview(hT_c, t0, n), r=hk(cur, g), w=[hkey_])
                norm_mod(n, gfin, None, lambda c: yf[:, c, :], [("yf", g % 2, c) for c in range(8)], hkey=hkey_, hbuf=hb, vkey="gfin")

            def T_():
                for t in range(4):
                    k = occ[0] % 2
                    occ[0] += 1
                    for half in range(2):
                        b = 1 + half
                        def f(e, t=t, half=half, b=b):
                            i = None
                            for c in range(4):
                                i = e.transpose(psb[b][:, c * 128:(c + 1) * 128], yf[:, half * 4 + c, t * 128:(t + 1) * 128], ident_f)
                            return i
                        pe_count[0] += 4
                        P.add("pe", f, r=[("yf", g % 2, half * 4 + c) for c in range(4)] + ["ident_f"], w=[PS(b)])
                        cp("dve" if half == 0 else "act", ot[k][:, half * 512:(half + 1) * 512], psb[b][:, :], w=[PS(b), ("ot", k, half)])
                    dma(out_d[t0 + t * 128:t0 + (t + 1) * 128, :], ot[k], r=[("ot", k, 0), ("ot", k, 1)], w=[("out", g, t)])
            return A_, T_
        fu = [mkf(g) for g in range(8)]
        fu[0][0]()
        for g in range(8):
            if g + 1 < 8:
                fu[g + 1][0]()
            fu[g][1]()
        A.release(mF)


    def pool_phase(l, with_ctx):
        nonlocal cur
        mark('pool%d' % l)
        j = l // 2
        mP = A.mark()
        prep_require(("modv", l, 2))
        vk = ("mod", l, 2)
        vk1 = ("mod", l, 1)
        wp = A.alloc([8, 256], BF16)
        band = A.alloc([4, 5, 128], BF16)
        xnb2 = [A.alloc([8, 512], BF16) for _ in range(2)]
        NH, NZ = 4, 4
        hin4 = [(hin, "hin")] + [(A.alloc([8, HW], F32), "hin%d" % (q_ + 2)) for q_ in range(NH - 1)]
        zt = [A.alloc([4, 1024], BF16) for _ in range(NZ)]
        hnew2 = [A.alloc([8, 512], F32) for _ in range(2)]
        prep_require(("wbpool", j))
        dma(wp, wbpool[j].rearrange("p (a d) -> p a d", a=8), r=[("wbpool", j)], w=["wp"])
        dma(band, band_d, w=["band"])
        src, dst = hT[cur], hT[1 - cur]
        glist = list(range(8)) + ([8] if with_ctx else [])
        zc = [0]
        yc = [0]

        def L_(g):
            t0, n = groups[g]
            hinb, hink = hin4[g % NH]
            dma(hinb[:, :, 0:n], hview(src, t0, n), r=hk(cur, g), w=[hink])

        def A_(g):
            t0, n = groups[g]
            col = 1 if g == 8 else 0
            par = g % 2
            hinb, hink = hin4[g % NH]
            norm_mod(n, gs1[l][:, col, :], modT[l][:, col, 0:8], lambda c: xnb2[par][:, c, 0:n], [("xnb", par, c) for c in range(8)],
                     hkey=hink, hbuf=hinb, vkey=vk1)

        def Z_(g):
            t0, n = groups[g]
            par = g % 2
            for t in range(n // 128):
                bz = ((1, 2), (3, 4))[zc[0] % 2]
                zc[0] += 1
                for hb_ in range(2):
                    items = []
                    for gg in (2 * hb_, 2 * hb_ + 1):
                        for kc in range(2):
                            items.append((psb[bz[hb_]][:, (gg % 2) * 256:(gg % 2 + 1) * 256], xnb2[par][:, 2 * gg + kc, t * 128:(t + 1) * 128],
                                          wp[:, 2 * gg + kc, :], kc == 0, kc == 1))
                    mm_group(items, r=["wp"] + [("xnb", par, c) for c in range(4 * hb_, 4 * hb_ + 4)], w=[PS(bz[hb_])])
                    cp("act" if hb_ == 0 else "dve", zt[g % NZ][:, t, hb_ * 512:(hb_ + 1) * 512], psb[bz[hb_]][:, :], w=[PS(bz[hb_]), ("zt", g % NZ, t, hb_)])
            prep_tick(1)

        def ztile(T):
            gi = T // 4
            return zt[gi % NZ][:, T % 4, :], [("zt", gi % NZ, T % 4, 0), ("zt", gi % NZ, T % 4, 1)]

        def Y_(g):
            t0, n = groups[g]
            col = 1 if g == 8 else 0
            par = g % 2
            hinb, hink = hin4[g % NH]
            hnew = hnew2[par]
            hnk = ("hnew", par)
            s0, s1 = (0, 32) if g < 8 else (32, 34)
            for d_ in range(8):
                wi = d_ // 2
                by = (5, 6, 7)[yc[0] % 3]
                yc[0] += 1
                items = []
                rkeys = ["band"]
                for t in range(n // 128):
                    T = t0 // 128 + t
                    parts = []
                    if T > s0:
                        parts.append((T - 1, 1))
                    parts.append((T, 3 if T == s0 else (4 if T == s1 - 1 else 0)))
                    if T < s1 - 1:
                        parts.append((T + 1, 2))
                    for i_, (Tz, kind) in enumerate(parts):
                        zv, zk = ztile(Tz)
                        rkeys += zk
                        items.append((psb[by][:, t * 128:(t + 1) * 128], zv[:, d_ * 128:(d_ + 1) * 128], band[:, wi, kind, :],
                                      i_ == 0, i_ == len(parts) - 1))
                mm_group(items, r=rkeys, w=[PS(by)])
                stt("dve", hnew[:, d_, 0:n], psb[by][:, 0:n], gp[l][:, col, d_:d_ + 1], hinb[:, d_, 0:n], ALU.mult, ALU.add,
                    r=[hink, vk], w=[PS(by), hnk])
            dma(hview(dst, t0, n), hnew[:, :, 0:n], r=[hnk], w=hk(1 - cur, g))
            prep_tick(1)

        NU = len(glist)
        L_(glist[0])
        for i_ in range(NU + 2):
            if i_ + 1 < NU:
                L_(glist[i_ + 1])
            if i_ < NU:
                A_(glist[i_])
            if 0 <= i_ - 2 < NU:
                Y_(glist[i_ - 2])
            if i_ < NU:
                Z_(glist[i_])
        cur = 1 - cur
        P.barrier()
        A.release(mP)

    def even_phase(l):
        j = l // 2
        upd_ctx = (l == 0)
        mark('E1_%d' % l)
        mE = A.mark()
        NTT = NT // 128
        U = A.alloc([NTT, 512], BF16)
        win = A.alloc([8, INW], BF16)
        mQ = A.mark()
        qT = A.alloc([4, NT], BF16)
        kT = A.alloc([2, NT], BF16)
        V = A.alloc([NTT, 2, 66], BF16)
        mW = A.mark()
        xnb = A.alloc([8, 512], BF16)
        cosb = A.alloc([512], F32)
        sinb = A.alloc([512], F32)
        t1s = [(tbuf[0], ("tb", 0)), (tbuf[1], ("tb", 1))]
        t2s = [(tbuf[2], ("tb", 2)), (A.alloc([512], F32), ("t2x", 0))]
        prep_require(("wbin", j))
        dma(win, wbin[j].rearrange("p (kc n) -> p kc n", kc=8), r=[("wbin", j)], w=["win"])
        P.add("pool", lambda e: e.memset(V[:, :, :, 64:66], 1.0), w=["V1"])
        hT_c = hT[cur]
        rc = 0
        xnbs = [xnb, sq[:, :, 0:512]]

        def e1_load(g):
            t0, n = groups[g]
            dma(hin[:, :, 0:n], hview(hT_c, t0, n), r=hk(cur, g), w=["hin"])

        def e1_norm(g):
            t0, n = groups[g]
            col = 1 if g == 8 else 0
            par = g % 2
            xb = xnbs[par]
            xk = [("xnb", par, c) for c in range(8)]
            norm_mod(n, gs1[l][:, col, :], modT[l][:, col, 0:8], lambda c: xb[:, c, 0:n], xk, vkey=("mod", l, 1), sqbuf=xb, sqkeys=xk)

        def e1_uv(g):
            t0, n = groups[g]
            isctx = g == 8
            par = g % 2
            xb = xnbs[par]
            xr = [("xnb", par, c) for c in range(8)]
            nt_ = n // 128
            if not isctx:
                dma(cosb[:, 0:n], cosT_d[:, t0:t0 + n], w=["cosb"])
                dma(sinb[:, 0:n], sinT_d[:, t0:t0 + n], w=["sinb"])
            for t in range(nt_):
                tile = t0 // 128 + t
                if (not isctx) or upd_ctx:
                    b = 1 + t % 2
                    mm_group([(psb[b][:, :], xb[:, kc, t * 128:(t + 1) * 128], win[:, kc, 0:512], kc == 0, kc == 7) for kc in range(8)],
                             r=xr + ["win"], w=[PS(b)])
                    cp("act", U[:, tile, :], psb[b][:, :], w=[PS(b), ("U", tile)])
                b = (7, 0)[t % 2]
                mm_group([(psb[b][:, 0:128], xb[:, kc, t * 128:(t + 1) * 128], win[:, kc, 2048:2176], kc == 0, kc == 7) for kc in range(8)],
                         r=xr + ["win"], w=[PS(b)])
                cp("dve", V[:, tile, :, 0:64], psb[b][:, 0:128].rearrange("p (g d) -> p g d", g=2), w=[PS(b), ("V", tile)])
            prep_tick(1)

        def e1_qk(g):
            nonlocal rc
            t0, n = groups[g]
            isctx = g == 8
            par = g % 2
            xb = xnbs[par]
            xr = [("xnb", par, c) for c in range(8)]
            specs = []
            if (not isctx) or upd_ctx:
                specs += [("q", qc, 512 + qc * 128, 1024 + qc * 128) for qc in range(4)]
            specs += [("k", kc2, 1536 + kc2 * 128, 1792 + kc2 * 128) for kc2 in range(2)]
            for si_, (kind, ci, c_main, c_sw) in enumerate(specs):
                bm, bs = ((3, 4), (5, 6))[si_ % 2]
                dstv = (qT if kind == "q" else kT)[:, ci, t0:t0 + n]
                dkey = (kind, ci, g)
                mm_group([(psb[bm][:, 0:n], win[:, kc, c_main:c_main + 128], xb[:, kc, 0:n], kc == 0, kc == 7) for kc in range(8)],
                         r=xr + ["win"], w=[PS(bm)])
                if isctx:
                    cp("act", dstv, psb[bm][:, 0:n], w=[PS(bm), dkey])
                    continue
                mm_group([(psb[bs][:, 0:n], win[:, kc, c_sw:c_sw + 128], xb[:, kc, 0:n], kc == 0, kc == 7) for kc in range(8)],
                         r=xr + ["win"], w=[PS(bs)])
                k2 = rc % 2
                rc += 1
                (t1a, t1k), (t2a, t2k) = t1s[k2], t2s[k2]
                tt("dve", t1a[:, 0:n], psb[bm][:, 0:n], cosb[:, 0:n], ALU.mult, r=["cosb"], w=[PS(bm), t1k])
                tt("dve", t2a[:, 0:n], psb[bs][:, 0:n], sinb[:, 0:n], ALU.mult, r=["sinb"], w=[PS(bs), t2k])
                tt("dve", dstv, t1a[:, 0:n], t2a[:, 0:n], ALU.add, r=[t1k, t2k], w=[dkey])
                if ci % 2 == 1:
                    prep_tick(1)

        e1_load(0)
        e1_norm(0)
        for g in range(9):
            if g + 1 < 9:
                e1_load(g + 1)
            e1_uv(g)
            if g + 1 < 9:
                e1_norm(g + 1)
            e1_qk(g)
        P.barrier()
        A.release(mW)
        if os.environ.get("EVSTOP") == "1":
            A.release(mE)
            return
        mark('E2_%d' % l)
        assert 8 * INW == 4 * NT
        YfT = win.rearrange("p a b -> p (a b)").rearrange("p (c t) -> p c t", c=4)
        mD = A.mark()
        NDB = 4
        dbuf = [A.alloc([4, 512], BF16) for _ in range(NDB)]
        pqb = [A.alloc([512], BF16) for _ in range(2)]
        dc_ = 0
        pc_ = 0
        orth = 1.0 / math.sqrt(S * 128.0)
        for kg in range(8):
            for q8 in range(8):
                k = dc_ % NDB
                dc_ += 1
                dma(dbuf[k], dft_d[kg, q8 * 512:(q8 + 1) * 512].rearrange("(t p) a k -> p t (a k)", p=128), w=[("dbuf", k)])
                for hd in range(4):
                    items = [(psb[1 + hd][:, :], U[:, q8 * 4 + t, hd * 128:(hd + 1) * 128], dbuf[k][:, t, :], (q8 == 0 and t == 0), (q8 == 7 and t == 3)) for t in range(4)]
                    mm_group(items, r=[("dbuf", k)] + [("U", q8 * 4 + t) for t in range(4)], w=[PS(1 + hd)])
                if q8 % 2 == 1:
                    prep_tick(1)
            for hd in range(4):
                k2 = pc_ % 2
                pc_ += 1
                cp("act" if hd % 2 == 0 else "dve", pqb[k2], psb[1 + hd][:, :], w=[PS(1 + hd), ("pq", k2)])
                b = 6 + hd % 2
                mm_group([(psb[b][:, 0:256], csc[:, 0, :], pqb[k2][:, 0:256], True, False),
                          (psb[b][:, 0:256], csc[:, 1, :], pqb[k2][:, 256:512], False, True),
                          (psb[b][:, 256:512], csc[:, 0, :], pqb[k2][:, 0:256], True, False),
                          (psb[b][:, 256:512], csc[:, 2, :], pqb[k2][:, 256:512], False, True)], r=[("pq", k2), "csc"], w=[PS(b)])
                act(YfT[:, hd, kg * 256:(kg + 1) * 256], psb[b][:, 0:256], AF.Identity, w=[PS(b), ("Yf", hd, kg)], scale=orth)
                if kg == 0:
                    act(YfT[:, hd, S - 1:S - 256:-1], psb[b][:, 257:512], AF.Identity, w=[PS(b), ("Yf", hd, 15)], scale=orth)
                else:
                    m0_ = S - kg * 256
                    act(YfT[:, hd, m0_:m0_ - 256:-1], psb[b][:, 256:512], AF.Identity, w=[PS(b), ("Yf", hd, 16 - kg), ("Yf", hd, 15 - kg)], scale=orth)
        pn = pqb[0][:, 0:8]
        for hd in range(4):
            mm_group([(psb[0][:, hd:hd + 1], U[:, t, hd * 128:(hd + 1) * 128], altv[:, 0:1], t == 0, t == 31) for t in range(32)],
                     r=[("U", t) for t in range(32)] + ["alt"], w=[PS(0)])
        cp("act", pn[:, 0:4], psb[0][:, 0:4], w=[PS(0), ("pq", 0)])
        for hd in range(4):
            mm_group([(psb[0][:, 8 + hd:9 + hd], csc[:, 0, :], pn[:, hd:hd + 1], True, True)], r=[("pq", 0), "csc"], w=[PS(0)])
        for hd in range(4):
            act(YfT[:, hd, S // 2:S // 2 + 1], psb[0][:, 8 + hd:9 + hd], AF.Identity, w=[PS(0), ("Yf", hd, 8)], scale=orth)
        if upd_ctx:
            orthc = 1.0 / math.sqrt(L * 128.0)
            for hd in range(4):
                k2 = pc_ % 2
                pc_ += 1
                mm_group([(psb[1 + hd][:, :], U[:, 32 + t, hd * 128:(hd + 1) * 128], dft256[:, t, :, :].rearrange("p a k -> p (a k)"), t == 0, t == 1) for t in range(2)],
                         r=[("U", 32), ("U", 33), "dft256"], w=[PS(1 + hd)])
                cp("act" if hd % 2 == 0 else "dve", pqb[k2], psb[1 + hd][:, :], w=[PS(1 + hd), ("pq", k2)])
                b = 6 + hd % 2
                mm_group([(psb[b][:, 0:256], csc[:, 0, :], pqb[k2][:, 0:256], True, False),
                          (psb[b][:, 0:256], csc[:, 1, :], pqb[k2][:, 256:512], False, True)], r=[("pq", k2), "csc"], w=[PS(b)])
                act(YfT[:, hd, S:S + 256], psb[b][:, 0:256], AF.Identity, w=[PS(b), ("Yf", hd, 16)], scale=orthc)
        P.barrier()
        A.release(mD)
        if os.environ.get("EVSTOP") == "2":
            A.release(mE)
            return
        mark('E3_%d' % l)
        attnT = U.rearrange("p t f -> p (t f)")[:, 0:4 * NT].rearrange("p (c t) -> p c t", c=4)
        mA = A.mark()
        NEB = 14
        Eb = [A.alloc([512], BF16) for _ in range(NEB)]
        atok = [A.alloc([512], BF16) for _ in range(2)]
        den = [A.alloc([4], F32) for _ in range(2)]
        cn = {"e": 0, "s": 0, "o": 0, "a": 0}
        qtiles = list(range(32)) + ([32, 33] if upd_ctx else [])
        pt_b = psb[7][:, :].bitcast(BF16)

        def make_unit(qb, g2, ak):
            isctx = qb >= 32
            if isctx:
                kbs = [(32, None), (33, None)]
            else:
                kbs = []
                if qb > 0:
                    kbs.append((qb - 1, maskL))
                kbs.append((qb, None))
                if qb < 31:
                    kbs.append((qb + 1, maskR))
                kbs += [(32, None), (33, None)]
            u = {"ets": [], "kbs": kbs, "qb": qb, "g2": g2, "ak": ak}
            u["po"] = 4 + cn["o"] % 2
            u["dk"] = cn["o"] % 2
            cn["o"] += 1

            def s_step(i_):
                kb, msk = kbs[i_]
                bA, bB = ((0, 1), (2, 3))[cn["s"] % 2]
                cn["s"] += 1
                ek = cn["e"] % NEB
                cn["e"] += 1
                gq = qb // 4 if qb < 32 else 8
                gk = kb // 4 if kb < 32 else 8
                items = []
                for jh in range(4):
                    h_ = 4 * g2 + jh
                    qc = h_ // 2
                    hf = h_ % 2
                    bb = bA if hf == 0 else bB
                    items.append((psb[bb][:, (jh // 2) * 128:(jh // 2 + 1) * 128], kT[hf * 64:(hf + 1) * 64, g2, kb * 128:(kb + 1) * 128],
                                  qT[hf * 64:(hf + 1) * 64, qc, qb * 128:(qb + 1) * 128], True, True))
                mm_group(items, r=[("k", g2, gk)] + [("q", qc_, gq) for qc_ in (2 * g2, 2 * g2 + 1)], w=[PS(bA), PS(bB)])
                Ev = Eb[ek].rearrange("p (a f q) -> p a f q", a=2, f=2)
                sin_ = ps_all[:, bA * 512:(bA + 2) * 512].rearrange("p (f x) -> p f x", f=2)[:, :, 0:256].rearrange("p f (a q) -> p f a q", a=2)
                act(Ev.transpose([0, 2, 1, 3]), sin_, AF.Exp, w=[PS(bA), PS(bB), ("E", ek)], scale=0.125)
                if msk is not None:
                    tt("dve", Eb[ek].rearrange("p (j q) -> p j q", j=4), Eb[ek].rearrange("p (j q) -> p j q", j=4),
                       msk.unsqueeze(1).to_broadcast([128, 4, 128]), ALU.mult, r=["maskL", "maskR"], w=[("E", ek)])
                u["ets"].append((ek, kb))

            def pv_step(jh):
                ets = u["ets"]
                po = u["po"]
                items = [(psb[po][:, jh * 66:(jh + 1) * 66], Eb[ek][:, jh * 128:(jh + 1) * 128], V[:, kb, g2, :], i_ == 0, i_ == len(ets) - 1)
                         for i_, (ek, kb) in enumerate(ets)]
                mm_group(items, r=[("E", ek) for (ek, _) in ets] + [("V", kb) for (_, kb) in ets] + ["V1"], w=[PS(po)])

            def fin():
                po, dk = u["po"], u["dk"]
                ov = psb[po][:, 0:264].rearrange("p (j d) -> p j d", d=66)
                tt("dve", den[dk], ov[:, :, 64], esink[:, j, 4 * g2:4 * g2 + 4], ALU.add, r=["esink"], w=[PS(po), ("den", dk)])
                P.add("dve", lambda e: e.reciprocal(den[dk], den[dk]), w=[("den", dk)])
                tt("dve", atok[ak][:, g2 * 256:(g2 + 1) * 256].rearrange("p (j d) -> p j d", d=64), ov[:, :, 0:64],
                   den[dk].unsqueeze(2).to_broadcast([128, 4, 64]), ALU.mult, r=[("den", dk)], w=[PS(po), ("atok", ak)])
                if g2 == 1:
                    def ftr(e):
                        i = None
                        for c in range(4):
                            i = e.transpose(pt_b[:, c * 128:(c + 1) * 128], atok[ak][:, c * 128:(c + 1) * 128], ident_b)
                        return i
                    pe_count[0] += 4
                    P.add("pe", ftr, r=[("atok", ak), "ident_b"], w=[PS(7)])
                    cp("dve", attnT[:, :, qb * 128:(qb + 1) * 128], pt_b[:, 0:512].rearrange("p (c q) -> p c q", c=4), w=[PS(7), ("attnT", qb)])
                    prep_tick(1, allow_ada=False)
            u["s"], u["pv"], u["fin"] = s_step, pv_step, fin
            return u

        prev = None
        for qb in qtiles:
            ak = cn["a"] % 2
            cn["a"] += 1
            for g2 in range(2):
                u = make_unit(qb, g2, ak)
                pvq = list(range(4)) if prev is not None else []
                for i_ in range(len(u["kbs"])):
                    u["s"](i_)
                    if i_ >= 1 and pvq:
                        prev["pv"](pvq.pop(0))
                while pvq:
                    prev["pv"](pvq.pop(0))
                if prev is not None:
                    prev["fin"]()
                prev = u
        for jh in range(4):
            prev["pv"](jh)
        prev["fin"]()
        P.barrier()
        A.release(mQ)
        if os.environ.get("EVSTOP") == "3":
            A.release(mE)
            return
        mark('E4_%d' % l)
        prep_require(("modv", l, 2))
        vk = ("mod", l, 2)
        wout = A.alloc([8, D], BF16)
        hin2 = A.alloc([8, HW], F32)
        hnew2 = [A.alloc([8, 512], F32) for _ in range(2)]
        prep_require(("wbout", j))
        dma(wout, wbout[j].rearrange("p (kc n) -> p kc n", kc=8), r=[("wbout", j)], w=["wout"])
        glist = list(range(8)) + ([8] if upd_ctx else [])

        def e4_load(g):
            t0, n = groups[g]
            hinb, hink = (hin, "hin") if g % 2 == 0 else (hin2, "hin2")
            dma(hinb[:, :, 0:n], hview(hT_c, t0, n), r=hk(cur, g), w=[hink])

        e4_load(glist[0])
        for gi_, g in enumerate(glist):
            t0, n = groups[g]
            col = 1 if g == 8 else 0
            hnew = hnew2[g % 2]
            hnk = ("hnew", g % 2)
            hinb, hink = (hin, "hin") if g % 2 == 0 else (hin2, "hin2")
            if gi_ + 1 < len(glist):
                e4_load(glist[gi_ + 1])
            for d_ in range(8):
                b = 1 + d_ % 6
                items = []
                for kc in range(8):
                    rhs = YfT[:, kc, t0:t0 + n] if kc < 4 else attnT[:, kc - 4, t0:t0 + n]
                    items.append((psb[b][:, 0:n], wout[:, kc, d_ * 128:(d_ + 1) * 128], rhs, kc == 0, kc == 7))
                mm_group(items, r=["wout"], w=[PS(b)])
                stt("dve", hnew[:, d_, 0:n], psb[b][:, 0:n], modT[l][:, col, 16 + d_:17 + d_], hinb[:, d_, 0:n], ALU.mult, ALU.add,
                    r=[hink, vk], w=[PS(b), hnk])
                if d_ % 4 == 3:
                    prep_tick(1)
            dma(hview(hT_c, t0, n), hnew[:, :, 0:n], r=[hnk], w=hk(cur, g))
        P.barrier()
        A.release(mE)

    for l in range(nlayers):
        upd = l < 2
        prep_require(("modv", l, 1))
        if enable_mix and l in mix_layers:
            if l % 2 == 0:
                even_phase(l)
            else:
                pool_phase(l, upd)
        if enable_ffn:
            ffn_phase(l, upd)
    final_phase()
    mark('end')
    P.finalize()

    with nc.Block() as block:
        @block.sync
        def _(h):
            P.emit("sp", h, psem, dsem)
            P.final_waits(h, dsem)

        @block.tensor
        def _(h):
            P.emit("pe", h, psem, dsem)

        @block.vector
        def _(h):
            P.emit("dve", h, psem, dsem)

        @block.scalar
        def _(h):
            P.emit("act", h, psem, dsem)

        @block.gpsimd
        def _(h):
            P.emit("pool", h, psem, dsem)
    st.close()
    build.peak = A.peak
    build.marks = marks
    return nc


def make_in_maps(inputs, cores):
    c = host_consts()
    cols = in_ext_cols()
    f32 = lambda a: np.ascontiguousarray(np.asarray(a, dtype=np.float32))
    x = f32(inputs["x"])
    cc = f32(inputs["c"])
    ctx = f32(inputs["ctx"])
    c_ctx = f32(inputs["c_ctx"])
    shared = {
        "ada_w": f32(inputs["ada_w"]),
        "adab": np.ascontiguousarray(f32(inputs["ada_b"]).reshape(DEPTH, 48, 128).transpose(2, 0, 1)),
        "gmix": fm(inputs["norm_mix_g"]),
        "gffn": fm(inputs["norm_ffn_g"]),
        "gfin": fm(inputs["final_g"]),
        "w_in": np.ascontiguousarray(f32(inputs["mix_in_w"])[:, :, cols]),
        "w_out": f32(inputs["mix_out_w"]),
        "sink": np.ascontiguousarray(np.broadcast_to(f32(inputs["attn_sink"])[None], (128, 2, 8))),
        "pool_w": f32(inputs["pool_w"]),
        "pscale": fm(inputs["pool_scale"]),
        "ffn_w1": f32(inputs["ffn_w1"]),
        "ffn_w3": f32(inputs["ffn_w3"]),
        "ffn_w2": f32(inputs["ffn_w2"]),
        "ident_f": c["ident_f"], "ident_b": c["ident_b"], "onesm": c["onesm"], "maskL": c["maskL"], "maskR": c["maskR"],
        "csc": c["csc"], "alt": c["alt"], "dft": c["dft"], "dft256": c["dft256"], "cosT": c["cosT"], "sinT": c["sinT"], "invc": c["invc"], "band": c["band"],
    }
    maps = []
    for b in cores:
        m = dict(shared)
        m["x"] = x[b]
        m["ctx"] = ctx[b]
        cv = np.stack([cc[b], c_ctx], axis=-1)
        m["cvec"] = np.ascontiguousarray(cv.reshape(8, 128, 2).transpose(1, 0, 2))
        maps.append(m)
    return maps


_NC_CACHE = {}


def kernel(**inputs):
    key = "full"
    if key not in _NC_CACHE:
        _NC_CACHE[key] = build()
    nc = _NC_CACHE[key]
    maps = make_in_maps(inputs, list(range(8)))
    res = run_bass_kernel_spmd(nc, maps, core_ids=list(range(8)))
    out = np.stack([np.asarray(r["out"], dtype=np.float32) for r in res.results], axis=0)
    return out
```
